# Optimizing a Trainium2 kernel written in Bass

```python
import jax
import jax.numpy as jnp
from jax import lax
import numpy as np

D_MODEL = 4096
BATCH = 2
SEQ = 4096
DEPTH = 4

N_MIXERS = 2
N_ATTN_LAYERS = (DEPTH + 1) // 2
N_RET_LAYERS = DEPTH // 2

ATTN_HEAD_DIM = 64
ATTN_Q_HEADS = D_MODEL // ATTN_HEAD_DIM
ATTN_KV_HEADS = 8
ATTN_GROUP = ATTN_Q_HEADS // ATTN_KV_HEADS
Q_DIM = ATTN_Q_HEADS * ATTN_HEAD_DIM
KV_DIM = ATTN_KV_HEADS * ATTN_HEAD_DIM
ATTN_QKV_DIM = Q_DIM + 2 * KV_DIM
WINDOW = 128
BLOCK = 128
ROPE_THETA = 500000.0
ROPE_DIM = ATTN_HEAD_DIM // 4

RET_HEADS = 16
RET_KEY_DIM = D_MODEL // RET_HEADS
RET_VALUE_FACTOR = 2
RET_VAL_DIM = RET_VALUE_FACTOR * RET_KEY_DIM
RET_V_WIDTH = RET_HEADS * RET_VAL_DIM
RET_PROJ_DIM = 2 * D_MODEL + 2 * RET_V_WIDTH
RET_CHUNK = 128
RET_ROT_BASE = 10000.0
GN_EPS = 1e-6

N_EXPERTS = 32
TOP_K = 4
EXPERT_FF = 256
SWIGLU_LIMIT = 7.0
SWIGLU_ALPHA = 1.702

DEEPNORM_ALPHA = (2 * DEPTH) ** 0.25
DEEPNORM_BETA = (8 * DEPTH) ** -0.25
LN_EPS = 1e-5
N_MOD = 6
ADALN_INIT_SCALE = 0.1
MAX_POS_OFFSET = 1024

kernel_name = 'hybrid_swa_sink_retention_moe_deepnorm_adaln'


def _layer_norm(x, gain, bias):
    xf = x.astype(jnp.float32)
    mu = jnp.mean(xf, axis=-1, keepdims=True)
    var = jnp.mean(jnp.square(xf - mu), axis=-1, keepdims=True)
    y = (xf - mu) * lax.rsqrt(var + LN_EPS) * gain.astype(jnp.float32) + bias.astype(jnp.float32)
    return y.astype(x.dtype)


def _rope_partial(x, cos, sin):
    half = ROPE_DIM // 2
    x1 = x[..., :half]
    x2 = x[..., half:ROPE_DIM]
    return jnp.concatenate([x1 * cos - x2 * sin, x2 * cos + x1 * sin, x[..., ROPE_DIM:]], axis=-1)


def _rotate_interleaved(x, cos, sin):
    xr = x.reshape(x.shape[:-1] + (x.shape[-1] // 2, 2))
    x0, x1 = xr[..., 0], xr[..., 1]
    return jnp.stack([x0 * cos - x1 * sin, x1 * cos + x0 * sin], axis=-1).reshape(x.shape)


def _window_mask(n_blocks):
    qi = jnp.arange(BLOCK)[:, None]
    kj = jnp.arange(2 * BLOCK)[None, :]
    diff = qi + BLOCK - kj
    band = (diff >= 0) & (diff < WINDOW)
    has_prev = (jnp.arange(n_blocks) > 0)[:, None, None]
    return band[None] & (has_prev | (kj >= BLOCK)[None])


def _sliding_window_sink_attention(u, w_qkv, b_qkv, sinks, w_o, b_o, cos, sin):
    bsz, seq, _ = u.shape
    nb = seq // BLOCK
    qkv = u @ w_qkv + b_qkv
    q, k, v = jnp.split(qkv, [Q_DIM, Q_DIM + KV_DIM], axis=-1)
    q = _rope_partial(q.reshape(bsz, seq, ATTN_Q_HEADS, ATTN_HEAD_DIM), cos, sin)
    k = _rope_partial(k.reshape(bsz, seq, ATTN_KV_HEADS, ATTN_HEAD_DIM), cos, sin)
    q = q.reshape(bsz, nb, BLOCK, ATTN_KV_HEADS, ATTN_GROUP, ATTN_HEAD_DIM)

    def band_keys(t):
        t = t.reshape(bsz, nb, BLOCK, ATTN_KV_HEADS, ATTN_HEAD_DIM)
        prev = jnp.concatenate([jnp.zeros_like(t[:, :1]), t[:, :-1]], axis=1)
        return jnp.concatenate([prev, t], axis=2)

    kb = band_keys(k)
    vb = band_keys(v)
    s = jnp.einsum('bnqhgd,bnkhd->bnhgqk', q, kb).astype(jnp.float32) * (ATTN_HEAD_DIM ** -0.5)
    s = jnp.where(_window_mask(nb)[None, :, None, None], s, -jnp.inf)
    sink = jnp.broadcast_to(
        sinks.astype(jnp.float32).reshape(1, 1, ATTN_KV_HEADS, ATTN_GROUP, 1, 1), s.shape[:-1] + (1,))
    p = jax.nn.softmax(jnp.concatenate([s, sink], axis=-1), axis=-1)[..., :-1]
    o = jnp.einsum('bnhgqk,bnkhd->bnqhgd', p.astype(vb.dtype), vb)
    return o.reshape(bsz, seq, Q_DIM) @ w_o + b_o


def _retention(u, w_qkvg, gn_gain, w_o, cos, sin):
    bsz, seq, _ = u.shape
    nc = seq // RET_CHUNK
    C = RET_CHUNK
    proj = u @ w_qkvg
    q, k, v, g = jnp.split(proj, [D_MODEL, 2 * D_MODEL, 2 * D_MODEL + RET_V_WIDTH], axis=-1)
    q = _rotate_interleaved(q.reshape(bsz, seq, RET_HEADS, RET_KEY_DIM), cos, sin)
    k = _rotate_interleaved(k.reshape(bsz, seq, RET_HEADS, RET_KEY_DIM), cos, sin) * (RET_KEY_DIM ** -0.5)
    qc = q.reshape(bsz, nc, C, RET_HEADS, RET_KEY_DIM)
    kc = k.reshape(bsz, nc, C, RET_HEADS, RET_KEY_DIM)
    vc = v.reshape(bsz, nc, C, RET_HEADS, RET_VAL_DIM)

    log_decay = jnp.log1p(-jnp.exp2(-5.0 - jnp.arange(RET_HEADS, dtype=jnp.float32)))
    idx = jnp.arange(C, dtype=jnp.float32)
    rel = idx[:, None] - idx[None, :]
    decay_intra = jnp.where(rel >= 0, jnp.exp(jnp.maximum(rel, 0.0)[None] * log_decay[:, None, None]), 0.0)
    xi = jnp.exp((idx + 1.0)[None, :] * log_decay[:, None]).T[None, :, :, None]
    zeta = jnp.exp((C - 1.0 - idx)[None, :] * log_decay[:, None]).T[None, :, :, None]
    decay_chunk = jnp.exp(C * log_decay)[None, :, None, None]

    s = jnp.einsum('bnihd,bnjhd->bnhij', qc, kc) * decay_intra
    o_intra = jnp.einsum('bnhij,bnjhe->bnihe', s, vc)

    def step(state, inp):
        q_i, k_i, v_i = inp
        o = jnp.einsum('bihd,bhde->bihe', q_i, state) * xi
        state = decay_chunk * state + jnp.einsum('bjhd,bjhe->bhde', k_i * zeta, v_i)
        return state, o

    state0 = jnp.zeros((bsz, RET_HEADS, RET_KEY_DIM, RET_VAL_DIM), jnp.float32)
    _, o_inter = lax.scan(step, state0, (jnp.moveaxis(qc, 1, 0), jnp.moveaxis(kc, 1, 0), jnp.moveaxis(vc, 1, 0)))
    o = (o_intra + jnp.moveaxis(o_inter, 0, 1)).astype(jnp.float32).reshape(bsz, seq, RET_HEADS, RET_VAL_DIM)

    mu = jnp.mean(o, axis=-1, keepdims=True)
    var = jnp.mean(jnp.square(o - mu), axis=-1, keepdims=True)
    o = ((o - mu) * lax.rsqrt(var + GN_EPS)).reshape(bsz, seq, RET_V_WIDTH) * gn_gain.astype(jnp.float32)
    return (jax.nn.silu(g) * o.astype(u.dtype)) @ w_o


def _moe(u, router_w, router_b, w_gu, b_gu, w_down, b_down):
    bsz, seq, d = u.shape
    h = u.reshape(bsz * seq, d)
    logits = (h @ router_w).astype(jnp.float32) + router_b.astype(jnp.float32)
    top_v, top_i = lax.top_k(logits, TOP_K)
    top_w = jax.nn.softmax(top_v, axis=-1)
    combine = jnp.einsum('tk,tke->te', top_w, jax.nn.one_hot(top_i, N_EXPERTS, dtype=jnp.float32)).astype(u.dtype)
    gu = jnp.einsum('td,edf->tef', h, w_gu) + b_gu
    gate = jnp.minimum(gu[..., :EXPERT_FF], SWIGLU_LIMIT)
    up = jnp.clip(gu[..., EXPERT_FF:], -SWIGLU_LIMIT, SWIGLU_LIMIT)
    act = (up + 1.0) * gate * jax.nn.sigmoid(SWIGLU_ALPHA * gate)
    y = jnp.einsum('tef,efd->td', act * combine[:, :, None], w_down) + combine @ b_down
    return y.reshape(bsz, seq, d)


def setup_inputs(seed: int = 0) -> dict:
    key = jax.random.key(seed)
    ks = jax.random.split(key, 24)
    f32 = jnp.float32

    def nrm(k, shape, scale):
        return jax.random.normal(k, shape, f32) * scale

    x = nrm(ks[0], (BATCH, SEQ, D_MODEL), 1.0)
    c = nrm(ks[1], (BATCH, D_MODEL), 1.0)
    offsets = jax.random.randint(ks[2], (BATCH, 1), 0, MAX_POS_OFFSET, dtype=jnp.int32)
    positions = (jnp.arange(SEQ, dtype=jnp.int32)[None, :] + offsets).astype(jnp.int32)

    mod_w = nrm(ks[3], (D_MODEL, N_MOD * D_MODEL), ADALN_INIT_SCALE * D_MODEL ** -0.5)
    mod_b = nrm(ks[4], (N_MOD * D_MODEL,), 0.01)
    mod_layer = nrm(ks[5], (DEPTH, N_MOD, D_MODEL), 0.02)
    ln_gain = 1.0 + nrm(ks[6], (DEPTH, 2, D_MODEL), 0.02)
    ln_bias = nrm(ks[7], (DEPTH, 2, D_MODEL), 0.01)

    attn_col = jnp.concatenate([jnp.ones((Q_DIM + KV_DIM,), f32), jnp.full((KV_DIM,), DEEPNORM_BETA, f32)]) * D_MODEL ** -0.5
    attn_w_qkv = jax.random.normal(ks[8], (N_ATTN_LAYERS, D_MODEL, ATTN_QKV_DIM), f32) * attn_col
    attn_b_qkv = nrm(ks[9], (N_ATTN_LAYERS, ATTN_QKV_DIM), 0.01)
    attn_sinks = nrm(ks[10], (N_ATTN_LAYERS, ATTN_Q_HEADS), 1.0)
    attn_w_o = nrm(ks[11], (N_ATTN_LAYERS, Q_DIM, D_MODEL), DEEPNORM_BETA * Q_DIM ** -0.5)
    attn_b_o = nrm(ks[12], (N_ATTN_LAYERS, D_MODEL), 0.01)

    ret_col = jnp.concatenate([jnp.ones((2 * D_MODEL,), f32), jnp.full((RET_V_WIDTH,), DEEPNORM_BETA, f32),
                               jnp.ones((RET_V_WIDTH,), f32)]) * D_MODEL ** -0.5
    ret_w_qkvg = jax.random.normal(ks[13], (N_RET_LAYERS, D_MODEL, RET_PROJ_DIM), f32) * ret_col
    ret_gn_gain = 1.0 + nrm(ks[14], (N_RET_LAYERS, RET_V_WIDTH), 0.02)
    ret_w_o = nrm(ks[15], (N_RET_LAYERS, RET_V_WIDTH, D_MODEL), DEEPNORM_BETA * RET_V_WIDTH ** -0.5)

    router_w = nrm(ks[16], (DEPTH, D_MODEL, N_EXPERTS), D_MODEL ** -0.5)
    router_b = nrm(ks[17], (DEPTH, N_EXPERTS), 0.01)
    expert_w_gu = nrm(ks[18], (DEPTH, N_EXPERTS, D_MODEL, 2 * EXPERT_FF), DEEPNORM_BETA * D_MODEL ** -0.5)
    expert_b_gu = nrm(ks[19], (DEPTH, N_EXPERTS, 2 * EXPERT_FF), 0.01)
    expert_w_down = nrm(ks[20], (DEPTH, N_EXPERTS, EXPERT_FF, D_MODEL), DEEPNORM_BETA * EXPERT_FF ** -0.5)
    expert_b_down = nrm(ks[21], (DEPTH, N_EXPERTS, D_MODEL), 0.01)

    return {'x': x, 'c': c, 'positions': positions, 'mod_w': mod_w, 'mod_b': mod_b, 'mod_layer': mod_layer,
            'ln_gain': ln_gain, 'ln_bias': ln_bias, 'attn_w_qkv': attn_w_qkv, 'attn_b_qkv': attn_b_qkv,
            'attn_sinks': attn_sinks, 'attn_w_o': attn_w_o, 'attn_b_o': attn_b_o, 'ret_w_qkvg': ret_w_qkvg,
            'ret_gn_gain': ret_gn_gain, 'ret_w_o': ret_w_o, 'router_w': router_w, 'router_b': router_b,
            'expert_w_gu': expert_w_gu, 'expert_b_gu': expert_b_gu, 'expert_w_down': expert_w_down,
            'expert_b_down': expert_b_down}


def reference(x, c, positions, mod_w, mod_b, mod_layer, ln_gain, ln_bias, attn_w_qkv, attn_b_qkv, attn_sinks,
              attn_w_o, attn_b_o, ret_w_qkvg, ret_gn_gain, ret_w_o, router_w, router_b, expert_w_gu, expert_b_gu,
              expert_w_down, expert_b_down):
    bsz = x.shape[0]
    mod = (jax.nn.silu(c) @ mod_w + mod_b).reshape(bsz, N_MOD, D_MODEL)

    pos = positions.astype(jnp.float32)[..., None]
    inv_a = ROPE_THETA ** (-jnp.arange(0, ROPE_DIM, 2, dtype=jnp.float32) / ROPE_DIM)
    ang_a = pos * inv_a
    cos_a = jnp.cos(ang_a)[:, :, None, :].astype(x.dtype)
    sin_a = jnp.sin(ang_a)[:, :, None, :].astype(x.dtype)
    inv_r = RET_ROT_BASE ** (-jnp.linspace(0.0, 1.0, RET_KEY_DIM // 2, dtype=jnp.float32))
    ang_r = pos * inv_r
    cos_r = jnp.cos(ang_r)[:, :, None, :].astype(x.dtype)
    sin_r = jnp.sin(ang_r)[:, :, None, :].astype(x.dtype)

    for i in range(DEPTH):
        m = mod + mod_layer[i]
        shift_t, scale_t, gate_t, shift_f, scale_f, gate_f = [m[:, j, None, :] for j in range(N_MOD)]
        j = i // N_MIXERS

        u = x * (1.0 + scale_t) + shift_t
        if i % N_MIXERS == 0:
            y = _sliding_window_sink_attention(u, attn_w_qkv[j], attn_b_qkv[j], attn_sinks[j], attn_w_o[j],
                                               attn_b_o[j], cos_a, sin_a)
        else:
            y = _retention(u, ret_w_qkvg[j], ret_gn_gain[j], ret_w_o[j], cos_r, sin_r)
        x = _layer_norm(DEEPNORM_ALPHA * x + (1.0 + gate_t) * y, ln_gain[i, 0], ln_bias[i, 0])

        u = x * (1.0 + scale_f) + shift_f
        y = _moe(u, router_w[i], router_b[i], expert_w_gu[i], expert_b_gu[i], expert_w_down[i], expert_b_down[i])
        x = _layer_norm(DEEPNORM_ALPHA * x + (1.0 + gate_f) * y, ln_gain[i, 1], ln_bias[i, 1])
    return x
```

```python
import math
import numpy as np
from contextlib import ExitStack
import concourse.bass as bass
import concourse.mybir as mybir
from concourse.bass_utils import run_bass_kernel_spmd

F32 = mybir.dt.float32
BF16 = mybir.dt.bfloat16
I32 = mybir.dt.int32
ALU = mybir.AluOpType
AF = mybir.ActivationFunctionType

SAME_ENGINE_SYNC = True

D = 4096
KC = D // 128
DEPTH = 4
QKV = 5120
RPROJ = 24576
RV = 8192
NEXP = 32
EFF = 256
ALPHA = (2 * DEPTH) ** 0.25
LN_EPS = 1e-5
GN_EPS = 1e-6
TWO_PI = 2.0 * math.pi
CW1 = 6.28125
CW2 = TWO_PI - CW1


class Buf:
    def __init__(self, h, name, dram=False):
        self.h = h
        self.name = name
        self.dram = dram
        self.last_w = None
        self.readers = []
        self.dsem = None
        self.dcount = 0
        self.last_dma = None

    def __getitem__(self, idx):
        return self.h[idx]


class Prog:
    ENGS = ["pe", "act", "dve", "pool", "sp"]

    def __init__(self):
        self.nc = bass.Bass("TRN2", target_bir_lowering=False)
        self.es = ExitStack()
        self.streams = {e: [] for e in self.ENGS}
        self.final_tokens = []

    def dram(self, name, shape, dtype, kind="Internal"):
        t = self.nc.dram_tensor(name, list(shape), dtype, kind=kind)
        return Buf(t.ap(), name, dram=True)

    def sb(self, name, shape, dtype):
        h = self.es.enter_context(self.nc.sbuf_tensor(name, list(shape), dtype))
        return Buf(h, name)

    def ps(self, name, shape, dtype):
        h = self.es.enter_context(self.nc.psum_tensor(name, list(shape), dtype))
        b = Buf(h, name)
        b.psum = True
        return b

    def _deps(self, r, w):
        deps = []
        for b in r:
            if b.last_w is not None:
                deps.append(b.last_w)
            if getattr(b, "psum", False):
                deps.extend(b.readers)
        for b in w:
            if b.last_w is not None:
                deps.append(b.last_w)
            deps.extend(b.readers)
        return deps

    def _commit(self, tok, r, w):
        key = (tok[0], tok[1] if tok[0] == "eng" else id(tok[1]))
        for b in r:
            b.readers = [t for t in b.readers if (t[0], t[1] if t[0] == "eng" else id(t[1])) != key]
            b.readers.append(tok)
        for b in w:
            b.last_w = tok
            b.readers = []

    def op(self, eng, fn, r=(), w=()):
        deps = self._deps(r, w)
        idx = len(self.streams[eng])
        tok = ("eng", eng, idx)
        self.streams[eng].append(dict(kind="op", fn=fn, deps=deps, signal=False))
        self._commit(tok, r, w)
        return tok

    def dma(self, eng, out_ap, in_ap, r=(), w=(), owner=None):
        if owner is None:
            cands = [b for b in list(w) + list(r) if not b.dram]
            owner = cands[0] if cands else (list(w) + list(r))[0]
        deps = self._deps(r, w)
        if owner.last_dma is not None:
            deps.append(owner.last_dma)
        if owner.dsem is None:
            owner.dsem = self.es.enter_context(self.nc.semaphore("d_" + owner.name))
        owner.dcount += 16
        tok = ("dma", owner, owner.dcount)
        owner.last_dma = tok
        self.streams[eng].append(dict(kind="dma", out=out_ap, in_=in_ap, deps=deps, sem=owner.dsem))
        self._commit(tok, r, w)
        return tok

    def view(self, parent, ap, name):
        return Buf(ap, name)

    def split(self, parent, children):
        for c in children:
            c.last_w = parent.last_w
            c.readers = list(parent.readers)

    def join(self, parent, children):
        toks = list(parent.readers)
        for c in children:
            if c.last_w is not None:
                toks.append(c.last_w)
            toks.extend(c.readers)
        parent.readers = toks

    def finish_on(self, tok):
        self.final_tokens.append(tok)

    def build(self):
        nc = self.nc
        for e in self.ENGS:
            for o in self.streams[e]:
                for d in o["deps"]:
                    if d[0] == "eng":
                        self.streams[d[1]][d[2]]["signal"] = True
        counts = {}
        for e in self.ENGS:
            c = 0
            for i, o in enumerate(self.streams[e]):
                if o.get("signal"):
                    c += 1
                    counts[(e, i)] = c
        esem = {e: self.es.enter_context(nc.semaphore("e_" + e)) for e in self.ENGS}
        block = self.es.enter_context(nc.Block())
        engobj = {"pe": "tensor", "act": "scalar", "dve": "vector", "pool": "gpsimd", "sp": "sync"}
        if self.final_tokens:
            self.streams["sp"].append(dict(kind="wait", deps=list(self.final_tokens)))

        def emit_stream(e, engine):
            known = {}
            for o in self.streams[e]:
                for d in o["deps"]:
                    if d[0] == "eng":
                        if d[1] == e and (e == "pe" or not SAME_ENGINE_SYNC):
                            continue
                        key = ("eng", d[1])
                        val = counts[(d[1], d[2])]
                        sem = esem[d[1]]
                    else:
                        key = ("dma", id(d[1]))
                        val = d[2]
                        sem = d[1].dsem
                    if known.get(key, 0) >= val:
                        continue
                    known[key] = val
                    engine.wait_ge(sem, val)
                if o["kind"] == "op":
                    ins = o["fn"](engine)
                    if o["signal"]:
                        ins.then_inc(esem[e], 1)
                elif o["kind"] == "dma":
                    engine.dma_start(out=o["out"], in_=o["in_"]).then_inc(o["sem"], 16)

        for e in self.ENGS:
            if self.streams[e]:
                getattr(block, engobj[e])(lambda engine, e=e: emit_stream(e, engine))
        self.es.close()
        return nc


def _bufs(*ops):
    return [o[0] for o in ops if isinstance(o, tuple)]


def _ap(o):
    return o[1] if isinstance(o, tuple) else o


def mm(P, out, lhsT, rhs, start, stop):
    o, l, r_ = out[1], lhsT[1], rhs[1]
    return P.op("pe", lambda e: e.matmul(o, l, r_, start=start, stop=stop), r=[lhsT[0], rhs[0]], w=[out[0]])


def tr(P, out, in_, ident):
    o, i, d = out[1], in_[1], ident[1]
    return P.op("pe", lambda e: e.transpose(o, i, d), r=[in_[0], ident[0]], w=[out[0]])


def act(P, out, in_, func, scale=1.0, bias=0.0):
    o, i, s, b = out[1], in_[1], _ap(scale), _ap(bias)
    return P.op("act", lambda e: e.activation(o, i, func, bias=b, scale=s),
                r=[in_[0]] + _bufs(scale, bias), w=[out[0]])


def tt(P, eng, out, in0, in1, op):
    o, a, b = out[1], in0[1], in1[1]
    return P.op(eng, lambda e: e.tensor_tensor(o, a, b, op), r=[in0[0], in1[0]], w=[out[0]])


def ts(P, eng, out, in0, s1, op0, s2=None, op1=None):
    o, a, x1, x2 = out[1], in0[1], _ap(s1), _ap(s2)
    if op1 is None:
        return P.op(eng, lambda e: e.tensor_scalar(o, a, x1, None, op0), r=[in0[0]] + _bufs(s1), w=[out[0]])
    return P.op(eng, lambda e: e.tensor_scalar(o, a, x1, x2, op0, op1), r=[in0[0]] + _bufs(s1, s2), w=[out[0]])


def stt(P, out, in0, scalar, in1, op0, op1):
    o, a, s, b = out[1], in0[1], _ap(scalar), in1[1]
    return P.op("dve", lambda e: e.scalar_tensor_tensor(o, a, s, b, op0, op1),
                r=[in0[0], in1[0]] + _bufs(scalar), w=[out[0]])


def cp(P, eng, out, in_):
    o, i = out[1], in_[1]
    if eng == "act":
        return P.op("act", lambda e: e.copy(o, i), r=[in_[0]], w=[out[0]])
    return P.op(eng, lambda e: e.tensor_copy(o, i), r=[in_[0]], w=[out[0]])


def memset(P, eng, out, val):
    o = out[1]
    return P.op(eng, lambda e: e.memset(o, val), r=[], w=[out[0]])


def dma(P, eng, out, in_):
    return P.dma(eng, out[1], in_[1], r=[in_[0]], w=[out[0]])


def W(buf, ap=None):
    return (buf, buf.h[:] if ap is None else ap)


class Model:
    def __init__(self, S, layers, do_mod=True, final=True):
        self.S = S
        self.layers = layers
        self.NT = S // 128
        self.TBL = min(1024, S)
        self.TBM = min(512, S)
        self.LNB = 256
        self.P = Prog()
        self.do_mod = do_mod
        self.final = final
        self.alloc()

    def alloc(self):
        P, S = self.P, self.S
        nl = len(self.layers)
        n_attn = sum(1 for i in self.layers if i % 2 == 0)
        n_ret = sum(1 for i in self.layers if i % 2 == 1)
        self.n_attn, self.n_ret = n_attn, n_ret
        EI = "ExternalInput"
        d = {}
        d["x"] = P.dram("x", [S, D], F32, EI)
        d["cT"] = P.dram("cT", [128, KC], F32, EI)
        d["pos"] = P.dram("pos", [128, self.NT], I32, EI)
        d["inv"] = P.dram("inv", [128, 136], F32, EI)
        d["mod_w"] = P.dram("mod_w", [96, 128, KC, 256], F32, EI)
        d["mod_b"] = P.dram("mod_b", [6, KC, 128], F32, EI)
        d["mod_layer"] = P.dram("mod_layer", [nl, 6, KC, 128], F32, EI)
        d["ln_gain"] = P.dram("ln_gain", [nl, 2, KC, 128], F32, EI)
        d["ln_bias"] = P.dram("ln_bias", [nl, 2, KC, 128], F32, EI)
        if n_attn:
            d["a_wqkv"] = P.dram("a_wqkv", [n_attn, 20, 128, KC, 256], F32, EI)
            d["a_bqkv"] = P.dram("a_bqkv", [n_attn, 1, QKV], F32, EI)
            d["a_sink"] = P.dram("a_sink", [n_attn, 128, KC], F32, EI)
            d["a_wo"] = P.dram("a_wo", [n_attn, 16, 128, KC, 256], F32, EI)
            d["a_bo"] = P.dram("a_bo", [n_attn, KC, 128], F32, EI)
            d["a_mask"] = P.dram("a_mask", [128, 2, 128], F32, EI)
        if n_ret:
            d["r_w"] = P.dram("r_w", [n_ret, 96, 128, KC, 256], F32, EI)
            d["r_gn"] = P.dram("r_gn", [n_ret, 128, 64], F32, EI)
            d["r_wo"] = P.dram("r_wo", [n_ret, 32, 128, 64, 128], F32, EI)
            d["r_dt"] = P.dram("r_dt", [128, 16, 128], F32, EI)
            d["r_xi"] = P.dram("r_xi", [128, 16, 128], F32, EI)
            d["r_zeta"] = P.dram("r_zeta", [128, 16], F32, EI)
            d["r_dc"] = P.dram("r_dc", [128, 16], F32, EI)
        d["m_rw"] = P.dram("m_rw", [nl, 128, KC, NEXP], F32, EI)
        d["m_rb"] = P.dram("m_rb", [nl, 1, NEXP], F32, EI)
        d["m_wgu"] = P.dram("m_wgu", [nl, NEXP, 2, 128, KC, 256], F32, EI)
        d["m_bgu"] = P.dram("m_bgu", [nl, 128, NEXP * 4], F32, EI)
        d["m_wdn"] = P.dram("m_wdn", [nl, 4, 8, 128, 16, 512], F32, EI)
        d["m_bdn"] = P.dram("m_bdn", [nl, NEXP, D], F32, EI)
        d["ident"] = P.dram("ident", [128, 128], F32, EI)
        d["out"] = P.dram("out", [S, D], F32, "ExternalOutput")
        d["xT0"] = P.dram("xT0", [D, S], F32)
        d["xT1"] = P.dram("xT1", [D, S], F32)
        d["yT"] = P.dram("yT", [D, S], F32)
        d["modv"] = P.dram("modv", [6, D], F32)
        d["rot"] = P.dram("rot", [S, 272], F32)
        d["qkv"] = P.dram("qkv", [S, RPROJ if n_ret else QKV], BF16)
        d["oT"] = P.dram("oT", [RV, S], BF16)
        d["cmbT"] = P.dram("cmbT", [NEXP, min(1024, S)], F32)
        self.d = d

        s = {}
        s["ATa"] = P.sb("ATa", [128, 16384], BF16)
        s["ATb"] = P.sb("ATb", [128, 16384], BF16)
        s["ws0"] = P.sb("ws0", [128, 8192], BF16)
        s["ws1"] = P.sb("ws1", [128, 8192], BF16)
        s["z"] = P.sb("z", [128, 8192], F32)
        s["st"] = [P.sb(f"st{i}", [128, 512], F32) for i in range(6)]
        s["sb16"] = [P.sb(f"sb16_{i}", [128, 1024], BF16) for i in range(6)]
        s["ident"] = P.sb("ident_s", [128, 128], F32)
        s["identb"] = P.sb("identb", [128, 128], BF16)
        s["ones"] = P.sb("ones_s", [128, 128], F32)
        s["vec"] = P.sb("vec", [128, 16, KC], F32)
        s["vraw"] = P.sb("vraw", [KC, 4, 128], F32)
        s["small"] = P.sb("small", [128, 1024], F32)
        s["rotA"] = P.sb("rotA", [128, self.NT, 16], F32)
        s["misc16"] = P.sb("misc16", [128, 4096], BF16)
        s["misc16b"] = P.sb("misc16b", [128, 3328], BF16)
        s["stA"] = P.sb("stA", [128, 2048], F32)
        s["stB"] = P.sb("stB", [128, 2048], F32)
        s["sbA"] = P.sb("sbA", [128, 2048], BF16)
        s["sbB"] = P.sb("sbB", [128, 2048], BF16)
        s["cst"] = P.sb("cst", [128, 1280], F32)
        self.s = s
        self.ps = [P.ps(f"ps{i}", [128, 512], F32) for i in range(8)]
        self.xcur = 0

    def ws(self, i):
        return self.s["ws0"] if i % 2 == 0 else self.s["ws1"]

    def load_consts(self):
        P, s, d = self.P, self.s, self.d
        dma(P, "sp", W(s["ident"]), W(d["ident"]))
        cp(P, "dve", W(s["identb"]), W(s["ident"]))
        memset(P, "dve", W(s["ones"]), 1.0)

    def vec(self, slot, kc=None):
        v = self.s["vec"]
        if kc is None:
            return (v, v.h[:, slot, :])
        return (v, v.h[:, slot, kc:kc + 1])

    def load_vec_T(self, slot_list):
        P, s = self.P, self.s
        vraw = s["vraw"]
        for (slot, terms, add_one) in slot_list:
            for j, (db, ap) in enumerate(terms):
                dma(P, "sp", (vraw, vraw.h[:, j, :]), (db, ap))
            acc = (vraw, vraw.h[:, 0, :])
            for j in range(1, len(terms)):
                tt(P, "dve", acc, acc, (vraw, vraw.h[:, j, :]), ALU.add)
            if add_one:
                ts(P, "dve", acc, acc, 1.0, ALU.add)
            pt = self.ps[7]
            tr(P, (pt, pt.h[:, 0:KC]), acc, (s["ident"], s["ident"].h[0:KC, 0:KC]))
            cp(P, "dve", self.vec(slot), (pt, pt.h[:, 0:KC]))

    def phase_mod(self):
        P, s, d = self.P, self.s, self.d
        sm = s["small"]
        cs = (sm, sm.h[:, 0:KC])
        dma(P, "sp", cs, W(d["cT"]))
        csb = (s["misc16"], s["misc16"].h[:, 0:KC])
        act(P, csb, cs, AF.Silu)
        orow = s["st"][0]
        nsl = 96
        dma(P, "pool", (self.ws(0), self.ws(0).h[:, :].rearrange("p (k n) -> p k n", k=KC)), (d["mod_w"], d["mod_w"].h[0]))
        for sl in range(nsl):
            if sl + 1 < nsl:
                wn = self.ws(sl + 1)
                dma(P, "pool", (wn, wn.h[:, :].rearrange("p (k n) -> p k n", k=KC)), (d["mod_w"], d["mod_w"].h[sl + 1]))
            wsl = self.ws(sl)
            pt = self.ps[sl % 2]
            for kc in range(KC):
                mm(P, (pt, pt.h[0:1, 0:256]), (csb[0], s["misc16"].h[:, kc:kc + 1]),
                   (wsl, wsl.h[:, kc * 256:(kc + 1) * 256]), kc == 0, kc == KC - 1)
            ob = s["st"][sl % 2]
            cp(P, "act", (ob, ob.h[0:1, 0:256]), (pt, pt.h[0:1, 0:256]))
            j, r = divmod(sl * 256, D)
            dma(P, "sp", (d["modv"], d["modv"].h[j:j + 1, r:r + 256]), (ob, ob.h[0:1, 0:256]))

    def phase_rot(self):
        P, s, d = self.P, self.s, self.d
        sm = s["small"]
        posi = s["cst"]
        pos_i = (posi, posi.h[:, 0:self.NT].bitcast(I32))
        dma(P, "sp", pos_i, W(d["pos"]))
        posf = (sm, sm.h[:, 64:64 + self.NT])
        cp(P, "dve", posf, pos_i)
        inv = (s["z"], s["z"].h[:, 0:136])
        dma(P, "sp", inv, W(d["inv"]))
        for t in range(self.NT):
            base = 256 + (t % 2) * 1024
            zz = s["z"]
            ang = (zz, zz.h[:, base:base + 136])
            kf = (zz, zz.h[:, base + 136:base + 272])
            ki = (zz, zz.h[:, base + 272:base + 408].bitcast(I32))
            r_ = (zz, zz.h[:, base + 408:base + 544])
            m_ = (zz, zz.h[:, base + 544:base + 680])
            rc = (zz, zz.h[:, base + 680:base + 816])
            ot = s["st"][t % 2]
            ts(P, "dve", ang, inv, (sm, sm.h[:, 64 + t:65 + t]), ALU.mult)
            ts(P, "dve", kf, ang, 1.0 / TWO_PI, ALU.mult)
            cp(P, "dve", ki, kf)
            cp(P, "dve", kf, ki)
            stt(P, r_, kf, -CW1, ang, ALU.mult, ALU.add)
            stt(P, r_, kf, -CW2, r_, ALU.mult, ALU.add)
            ts(P, "dve", m_, r_, math.pi, ALU.is_gt)
            stt(P, r_, m_, -TWO_PI, r_, ALU.mult, ALU.add)
            ts(P, "dve", m_, r_, -math.pi, ALU.is_lt)
            stt(P, r_, m_, TWO_PI, r_, ALU.mult, ALU.add)
            ts(P, "dve", rc, r_, math.pi / 2, ALU.add)
            ts(P, "dve", m_, rc, math.pi, ALU.is_gt)
            stt(P, rc, m_, -TWO_PI, rc, ALU.mult, ALU.add)
            ts(P, "dve", r_, r_, math.pi, ALU.min, -math.pi, ALU.max)
            ts(P, "dve", rc, rc, math.pi, ALU.min, -math.pi, ALU.max)
            act(P, (ot, ot.h[:, 0:8]), (zz, rc[1][:, 0:8]), AF.Sin)
            act(P, (ot, ot.h[:, 8:16]), (zz, r_[1][:, 0:8]), AF.Sin)
            act(P, (ot, ot.h[:, 16:144]), (zz, rc[1][:, 8:136]), AF.Sin)
            act(P, (ot, ot.h[:, 144:272]), (zz, r_[1][:, 8:136]), AF.Sin)
            dma(P, "sp", (d["rot"], d["rot"].h[t * 128:(t + 1) * 128, :]), (ot, ot.h[:, 0:272]))
        rotA = s["rotA"]
        dma(P, "sp", W(rotA), (d["rot"], d["rot"].h[:, 0:16].rearrange("(t p) c -> p t c", p=128)))

    def phase_xT(self):
        P, s, d = self.P, self.s, self.d
        xt = s["z"]
        for t in range(self.NT):
            half = (t % 2) * 4096
            xtile = (xt, xt.h[:, half:half + 4096])
            dma(P, "sp", xtile, (d["x"], d["x"].h[t * 128:(t + 1) * 128, :]))
            for g in range(KC // 4):
                pt = self.ps[g % 2]
                for j in range(4):
                    kc = g * 4 + j
                    tr(P, (pt, pt.h[:, j * 128:(j + 1) * 128]), (xt, xt.h[:, half + kc * 128:half + (kc + 1) * 128]), W(s["ident"]))
                ob = s["st"][g % 4]
                cp(P, "act" if g % 2 == 0 else "dve", W(ob), W(pt))
                dst = d["xT0"].h[g * 512:(g + 1) * 512, t * 128:(t + 1) * 128].rearrange("(k p) q -> p k q", p=128)
                dma(P, "sp", (d["xT0"], dst), (ob, ob.h[:, :].rearrange("p (k q) -> p k q", k=4)))
        self.xcur = 0

    def xT(self, which):
        return self.d["xT0"] if which == 0 else self.d["xT1"]

    def prologue(self, dst, dst_stride, tok0, ntok, ln, sl_g, sl_gb, sl_A, sl_B, sl_gain, sl_bias,
                 router=None, write_x=True):
        P, s, d = self.P, self.s, self.d
        LNB = min(self.LNB, ntok)
        xin = self.xT(self.xcur)
        xout = self.xT(1 - self.xcur)
        z = s["z"]
        dstfn = dst
        for b0 in range(0, ntok, LNB):
            c0 = tok0 + b0
            if ln:
                ps_s, ps_q = self.ps[4], self.ps[5]
                for kc in range(KC):
                    xs = s["st"][kc % 2]
                    ys = s["st"][2 + kc % 2]
                    sq = s["st"][4 + kc % 2]
                    dma(P, "sp", (xs, xs.h[:, 0:LNB]), (xin, xin.h[kc * 128:(kc + 1) * 128, c0:c0 + LNB]))
                    dma(P, "sp", (ys, ys.h[:, 0:LNB]), (d["yT"], d["yT"].h[kc * 128:(kc + 1) * 128, c0:c0 + LNB]))
                    act(P, (ys, ys.h[:, 0:LNB]), (ys, ys.h[:, 0:LNB]), AF.Identity, scale=self.vec(sl_g, kc), bias=self.vec(sl_gb, kc))
                    zk = (z, z.h[:, kc * LNB:(kc + 1) * LNB])
                    stt(P, zk, (xs, xs.h[:, 0:LNB]), ALPHA, (ys, ys.h[:, 0:LNB]), ALU.mult, ALU.add)
                    tt(P, "pool", (sq, sq.h[:, 0:LNB]), zk, zk, ALU.mult)
                    mm(P, (ps_s, ps_s.h[:, 0:LNB]), W(s["ones"]), zk, kc == 0, kc == KC - 1)
                    mm(P, (ps_q, ps_q.h[:, 0:LNB]), W(s["ones"]), (sq, sq.h[:, 0:LNB]), kc == 0, kc == KC - 1)
                sm = s["small"]
                mean = (sm, sm.h[:, 128:128 + LNB])
                rstd = (sm, sm.h[:, 384:384 + LNB])
                tmp = (sm, sm.h[:, 640:640 + LNB])
                ts(P, "dve", mean, (ps_s, ps_s.h[:, 0:LNB]), 1.0 / D, ALU.mult)
                ts(P, "dve", rstd, (ps_q, ps_q.h[:, 0:LNB]), 1.0 / D, ALU.mult)
                tt(P, "dve", tmp, mean, mean, ALU.mult)
                tt(P, "dve", rstd, rstd, tmp, ALU.subtract)
                ts(P, "dve", rstd, rstd, 0.0, ALU.max, LN_EPS, ALU.add)
                act(P, rstd, rstd, AF.Sqrt)
                P.op("dve", lambda e, o=rstd[1], i=rstd[1]: e.reciprocal(o, i), r=[sm], w=[sm])
            for kc in range(KC):
                if ln:
                    zk = (z, z.h[:, kc * LNB:(kc + 1) * LNB])
                    tt(P, "dve", zk, zk, mean, ALU.subtract)
                    tt(P, "dve", zk, zk, rstd, ALU.mult)
                    if write_x:
                        xo = s["st"][kc % 2]
                        ts(P, "pool", (xo, xo.h[:, 0:LNB]), zk, self.vec(sl_gain, kc), ALU.mult, self.vec(sl_bias, kc), ALU.add)
                        dma(P, "sp", (xout, xout.h[kc * 128:(kc + 1) * 128, c0:c0 + LNB]), (xo, xo.h[:, 0:LNB]))
                    src = zk
                else:
                    xs = s["st"][kc % 2]
                    dma(P, "sp", (xs, xs.h[:, 0:LNB]), (xin, xin.h[kc * 128:(kc + 1) * 128, c0:c0 + LNB]))
                    src = (xs, xs.h[:, 0:LNB])
                if dstfn is not None:
                    dd = dstfn(kc, b0, LNB)
                    if router is None:
                        act(P, dd, src, AF.Identity, scale=self.vec(sl_A, kc), bias=self.vec(sl_B, kc))
                    else:
                        uf = s["st"][2 + kc % 2]
                        act(P, (uf, uf.h[:, 0:LNB]), src, AF.Identity, scale=self.vec(sl_A, kc), bias=self.vec(sl_B, kc))
                        cp(P, "dve", dd, (uf, uf.h[:, 0:LNB]))
                        rw32, ps_list, _cb = router
                        for tl in range(LNB // 128):
                            pr = ps_list[tl % 2]
                            mm(P, (pr, pr.h[:, 0:NEXP]), (uf, uf.h[:, tl * 128:(tl + 1) * 128]),
                               (rw32, rw32.h[:, kc * NEXP:(kc + 1) * NEXP]), kc == 0, kc == KC - 1)
            if router is not None:
                router[2](b0, LNB)
        return

    def flip_x(self):
        self.xcur = 1 - self.xcur

    def mv(self, j):
        m = self.d["modv"]
        return (m, m.h[j:j + 1, :].rearrange("o (k p) -> (o k) p", p=128))

    def setup_vectors(self, li, i, first_sublayer, prev):
        P, s, d = self.P, self.s, self.d
        j0 = 0 if first_sublayer else 3
        mv, mb, ml = d["modv"], d["mod_b"], d["mod_layer"]
        items = []
        items.append((6, [self.mv(j0 + 1), (mb, mb.h[j0 + 1]), (ml, ml.h[li, j0 + 1])], True))
        items.append((7, [self.mv(j0), (mb, mb.h[j0]), (ml, ml.h[li, j0])], False))
        if prev is not None:
            pli, pk, pj = prev["li"], prev["k"], prev["gate"]
            items.append((0, [self.mv(pj), (mb, mb.h[pj]), (ml, ml.h[pli, pj])], True))
            items.append((4, [(d["ln_gain"], d["ln_gain"].h[pli, pk])], False))
            items.append((5, [(d["ln_bias"], d["ln_bias"].h[pli, pk])], False))
            if prev.get("bo") is not None:
                items.append((8, [prev["bo"]], False))
        self.load_vec_T(items)
        if prev is not None:
            if prev.get("bo") is not None:
                tt(P, "dve", self.vec(1), self.vec(0), self.vec(8), ALU.mult)
            else:
                memset(P, "dve", self.vec(1), 0.0)
            tt(P, "dve", self.vec(2), self.vec(4), self.vec(6), ALU.mult)
            tt(P, "dve", self.vec(3), self.vec(5), self.vec(6), ALU.mult)
            tt(P, "dve", self.vec(3), self.vec(3), self.vec(7), ALU.add)
        else:
            cp(P, "dve", self.vec(2), self.vec(6))
            cp(P, "dve", self.vec(3), self.vec(7))

    def at_split(self, kc, tok, n):
        b = self.s["ATa"] if tok < 512 else self.s["ATb"]
        c = kc * 512 + (tok % 512)
        return (b, b.h[:, c:c + n])

    def linear_tok(self, w_dram_fn, n_slabs, ntile, epilogue, lhs_fn, kch=KC, extra=None):
        P, s = self.P, self.s
        dma(P, "pool", (self.ws(0), self.ws(0).h[:, 0:kch * 256].rearrange("p (k n) -> p k n", k=kch)), w_dram_fn(0))
        for sl in range(n_slabs):
            if sl + 1 < n_slabs:
                wn = self.ws(sl + 1)
                dma(P, "pool", (wn, wn.h[:, 0:kch * 256].rearrange("p (k n) -> p k n", k=kch)), w_dram_fn(sl + 1))
            wsl = self.ws(sl)
            for t in range(ntile):
                pt = self.ps[t % 4]
                for kc in range(kch):
                    last = (kc == kch - 1) and extra is None
                    mm(P, (pt, pt.h[:, 0:256]), lhs_fn(kc, t), (wsl, wsl.h[:, kc * 256:(kc + 1) * 256]), kc == 0, last)
                if extra is not None:
                    extra(sl, t, pt)
                epilogue(sl, t, pt)

    def linear_feat(self, w_dram_fn, n_slabs, slab_cols, nblk, blk, kch, epilogue, rhs_fn, extra=None):
        P, s = self.P, self.s
        ncs = slab_cols // 128
        dma(P, "pool", (self.ws(0), self.ws(0).h[:, 0:kch * slab_cols].rearrange("p (k n) -> p k n", k=kch)), w_dram_fn(0))
        cnt = 0
        for sl in range(n_slabs):
            if sl + 1 < n_slabs:
                wn = self.ws(sl + 1)
                dma(P, "pool", (wn, wn.h[:, 0:kch * slab_cols].rearrange("p (k n) -> p k n", k=kch)), w_dram_fn(sl + 1))
            wsl = self.ws(sl)
            for c in range(ncs):
                pts = [self.ps[(cnt + b) % 4] for b in range(nblk)]
                cnt += nblk
                for kc in range(kch):
                    last = (kc == kch - 1) and extra is None
                    for b in range(nblk):
                        mm(P, (pts[b], pts[b].h[:, 0:blk]), (wsl, wsl.h[:, kc * slab_cols + c * 128:kc * slab_cols + (c + 1) * 128]),
                           rhs_fn(kc, b), kc == 0, last)
                for b in range(nblk):
                    if extra is not None:
                        extra(sl * ncs + c, b, pts[b])
                    epilogue(sl * ncs + c, b, pts[b])

    def attention(self, li, ai, prev):
        P, s, d = self.P, self.s, self.d
        S, NT, TB = self.S, self.NT, self.TBL
        self.setup_vectors(li, li, True, prev)
        brow = (s["misc16b"], s["misc16b"].h[0:1, 0:QKV]) if False else None
        bqa, bqb = s["sbA"], s["sbB"]
        dma(P, "pool", (bqa, bqa.h[0:1, 0:2048]), (d["a_bqkv"], d["a_bqkv"].h[ai, :, 0:2048]))
        dma(P, "pool", (bqb, bqb.h[0:1, 0:2048]), (d["a_bqkv"], d["a_bqkv"].h[ai, :, 2048:4096]))
        bq2 = s["misc16b"]
        dma(P, "pool", (bq2, bq2.h[0:1, 0:1024]), (d["a_bqkv"], d["a_bqkv"].h[ai, :, 4096:QKV]))
        onesb = (s["misc16b"], s["misc16b"].h[0:1, 1024:1152])
        memset(P, "dve", onesb, 1.0)
        rotA = s["rotA"]

        for blk in range(S // TB):
            tok0 = blk * TB
            self.prologue(lambda kc, b0, n: self.at_split(kc, b0, n), TB, tok0, TB, prev is not None, 0, 1, 2, 3, 4, 5)
            if False:
                pass

            def extra(sl, t, pt):
                c0 = sl * 256
                if c0 < 2048:
                    rhs = (bqa, bqa.h[0:1, c0:c0 + 256])
                elif c0 < 4096:
                    rhs = (bqb, bqb.h[0:1, c0 - 2048:c0 - 2048 + 256])
                else:
                    rhs = (bq2, bq2.h[0:1, c0 - 4096:c0 - 4096 + 256])
                mm(P, (pt, pt.h[:, 0:256]), onesb, rhs, False, True)

            def epi(sl, t, pt, tok0=tok0):
                gt = tok0 // 128 + t
                qk = s["st"][t % 2]
                ob = s["sb16"][t % 2]
                if sl < 18:
                    cp(P, "act", (qk, qk.h[:, 0:256]), (pt, pt.h[:, 0:256]))
                    v3 = qk.h[:, 0:256].rearrange("p (h e) -> p h e", h=4)
                    x1, x2 = (qk, v3[:, :, 0:8]), (qk, v3[:, :, 8:16])
                    cosb = (rotA, rotA.h[:, gt, 0:8].unsqueeze(1).to_broadcast([128, 4, 8]))
                    sinb = (rotA, rotA.h[:, gt, 8:16].unsqueeze(1).to_broadcast([128, 4, 8]))
                    tmp = s["st"][4 + t % 2]
                    t3 = tmp.h[:, 0:128].rearrange("p (a h e) -> p a h e", a=4, h=4)
                    T1, T2, T3, T4 = [(tmp, t3[:, a]) for a in range(4)]
                    tt(P, "dve", T1, x1, cosb, ALU.mult)
                    tt(P, "dve", T2, x2, sinb, ALU.mult)
                    tt(P, "dve", T3, x2, cosb, ALU.mult)
                    tt(P, "dve", T4, x1, sinb, ALU.mult)
                    tt(P, "dve", x1, T1, T2, ALU.subtract)
                    tt(P, "dve", x2, T3, T4, ALU.add)
                    cp(P, "act", (ob, ob.h[:, 0:256]), (qk, qk.h[:, 0:256]))
                else:
                    cp(P, "act", (ob, ob.h[:, 0:256]), (pt, pt.h[:, 0:256]))
                dma(P, "sp", (d["qkv"], d["qkv"].h[gt * 128:(gt + 1) * 128, sl * 256:(sl + 1) * 256]), (ob, ob.h[:, 0:256]))

            self.linear_tok(lambda sl: (d["a_wqkv"], d["a_wqkv"].h[ai, sl]), 20, TB // 128, epi,
                            lambda kc, t: self.at_split(kc, t * 128, 128), extra=extra)
        if prev is not None:
            self.flip_x()

        cst = s["cst"]
        maskf = (cst, cst.h[:, 0:256].rearrange("p (a q) -> p a q", a=2))
        dma(P, "sp", maskf, W(d["a_mask"]))
        maskb = s["sbA"]
        mb3 = maskb.h[:, 0:256].rearrange("p (a q) -> p a q", a=2)
        cp(P, "dve", (maskb, mb3), maskf)
        es = (cst, cst.h[:, 256:256 + KC])
        dma(P, "sp", es, (d["a_sink"], d["a_sink"].h[ai]))
        act(P, es, es, AF.Exp)
        op_ = s["sbA"]
        OPv = op_.h[:, 256:512].rearrange("p (a c) -> p a c", a=2)
        memset(P, "dve", (op_, OPv), 0.0)
        memset(P, "dve", (op_, OPv[:, 0, 0:64]), 1.0)
        memset(P, "dve", (op_, OPv[:, 1, 64:128]), 1.0)
        vp = s["stA"]
        vpb = vp.h[:, :].bitcast(BF16)
        VP = vpb.rearrange("p (s a g c) -> p s a g c", s=2, a=2, g=8)
        memset(P, "pool", (vp, vpb), 0.0)
        ktr = s["misc16b"]
        KT = ktr.h[:, 1280:1280 + 2048].rearrange("p (s g k) -> p s g k", s=2, g=8)
        kd = s["sb16"][4]
        qtile = s["z"]
        qt16 = qtile.h[:, :].bitcast(BF16)
        QT = s["misc16"]
        for blk in range(S // TB):
            for qi in range(TB // 128):
                gt = blk * (TB // 128) + qi
                slot = gt % 2
                qh = (gt % 2) * 8192
                qv = qt16[:, qh:qh + 4096]
                kv = qt16[:, qh + 4096:qh + 4608]
                vv = qt16[:, qh + 4608:qh + 5120]
                dma(P, "sp", (qtile, qt16[:, qh:qh + QKV]), (d["qkv"], d["qkv"].h[gt * 128:(gt + 1) * 128, 0:QKV]))
                kd4 = kd.h[:, 0:1024].rearrange("p (g a e) -> p g a e", g=8, a=2)
                cp(P, "pool", (kd, kd4), (qtile, kv.rearrange("p (g e) -> p g e", g=8).unsqueeze(2).to_broadcast([128, 8, 2, 64])))
                ptk = self.ps[6]
                ptk16 = ptk.h[:, :].bitcast(BF16)
                for g in range(8):
                    tr(P, (ptk, ptk16[:, g * 128:(g + 1) * 128]), (kd, kd.h[:, g * 128:(g + 1) * 128]), W(s["identb"]))
                cp(P, "act", (ktr, KT[:, slot]), (ptk, ptk16.rearrange("p (g k) -> p g k", g=8)))
                v3 = vv.rearrange("p (g e) -> p g e", g=8)
                cp(P, "pool", (vp, VP[:, slot, 0, :, 0:64]), (qtile, v3))
                cp(P, "pool", (vp, VP[:, slot, 1, :, 64:128]), (qtile, v3))
                for gq in range(4):
                    ptq = self.ps[7]
                    ptq16 = ptq.h[:, :].bitcast(BF16)
                    for j in range(8):
                        c = gq * 8 + j
                        tr(P, (ptq, ptq16[:, j * 128:(j + 1) * 128]), (qtile, qv[:, c * 128:(c + 1) * 128]), W(s["identb"]))
                    cp(P, "dve" if gq % 2 else "act", (QT, QT.h[:, gq * 1024:(gq + 1) * 1024]), (ptq, ptq16))
                kts = [(1 - slot, 0), (slot, 1)] if gt > 0 else [(slot, 1)]
                for g in range(8):
                    Es = []
                    cnt = 0
                    for pi in range(2):
                        for (ks, mi) in kts:
                            pS = self.ps[cnt % 4]
                            lo, hi = pi * 64, pi * 64 + 64
                            mm(P, (pS, pS.h[:, :]), (ktr, KT[lo:hi, ks, g, :]),
                               (QT, QT.h[lo:hi, 4 * g * 128:(4 * g + 4) * 128]), True, True)
                            E = s["sb16"][cnt % 4]
                            act(P, (E, E.h[:, 0:512]), (pS, pS.h[:, :]), AF.Exp, scale=0.125)
                            E3 = E.h[:, 0:512].rearrange("p (j q) -> p j q", j=4)
                            tt(P, "pool", (E, E3), (E, E3), (maskb, mb3[:, mi, :].unsqueeze(1).to_broadcast([128, 4, 128])), ALU.mult)
                            Es.append((E, pi, ks))
                            cnt += 1
                    pO, pD = self.ps[4], self.ps[5]
                    for n_, (E, pi, ks) in enumerate(Es):
                        mm(P, (pO, pO.h[:, :]), (vp, VP[:, ks, pi, g, :]), (E, E.h[:, 0:512]), n_ == 0, n_ == len(Es) - 1)
                    for n_, (E, pi, ks) in enumerate(Es):
                        mm(P, (pD, pD.h[:, :]), (op_, OPv[:, pi, :]), (E, E.h[:, 0:512]), n_ == 0, n_ == len(Es) - 1)
                    den = s["st"][g % 2]
                    d3 = den.h[:, 0:512].rearrange("p (j q) -> p j q", j=4)
                    tt(P, "dve", (den, d3), (pD, pD.h[:, :].rearrange("p (j q) -> p j q", j=4)),
                       (cst, cst.h[:, 256 + 4 * g:256 + 4 * g + 4].unsqueeze(2).to_broadcast([128, 4, 128])), ALU.add)
                    P.op("dve", lambda e, o=den.h[:, 0:512], i=den.h[:, 0:512]: e.reciprocal(o, i), r=[den], w=[den])
                    for j in range(4):
                        c = 4 * g + j
                        tt(P, "dve", self.at_split(c, qi * 128, 128),
                           (pO, pO.h[:, j * 128:(j + 1) * 128]), (den, den.h[:, j * 128:(j + 1) * 128]), ALU.mult)
            nb = TB // 512 if TB >= 512 else 1
            bw = min(512, TB)

            def epi_o(cc, b, pt, blk=blk, bw=bw):
                ob = s["st"][2 + cc % 2]
                cp(P, "act", (ob, ob.h[:, 0:bw]), (pt, pt.h[:, 0:bw]))
                c0 = blk * TB + b * bw
                dma(P, "sp", (d["yT"], d["yT"].h[cc * 128:(cc + 1) * 128, c0:c0 + bw]), (ob, ob.h[:, 0:bw]))

            self.linear_feat(lambda sl: (d["a_wo"], d["a_wo"].h[ai, sl]), 16, 256, nb, bw, KC, epi_o,
                             lambda kc, b, bw=bw: self.at_split(kc, b * 512, bw))

    def retention(self, li, ri, prev):
        P, s, d = self.P, self.s, self.d
        S, NT, TB = self.S, self.NT, self.TBL
        self.setup_vectors(li, li, True, prev)
        rotR = s["stB"]
        for blk in range(S // TB):
            tok0 = blk * TB
            self.prologue(lambda kc, b0, n: self.at_split(kc, b0, n), TB, tok0, TB, prev is not None, 0, 1, 2, 3, 4, 5)
            for t in range(TB // 128):
                gt = tok0 // 128 + t
                dma(P, "sp", (rotR, rotR.h[:, t * 256:(t + 1) * 256]), (d["rot"], d["rot"].h[gt * 128:(gt + 1) * 128, 16:272]))

            def epi(sl, t, pt, tok0=tok0):
                gt = tok0 // 128 + t
                ob = s["sb16"][t % 2]
                if sl < 32:
                    qk = s["st"][t % 2]
                    cp(P, "act", (qk, qk.h[:, 0:256]), (pt, pt.h[:, 0:256]))
                    v3 = qk.h[:, 0:256].rearrange("p (e two) -> p e two", two=2)
                    x0, x1 = (qk, v3[:, :, 0]), (qk, v3[:, :, 1])
                    cosb = (rotR, rotR.h[:, t * 256:t * 256 + 128])
                    sinb = (rotR, rotR.h[:, t * 256 + 128:t * 256 + 256])
                    tmp = s["st"][4 + t % 2]
                    T1, T2, T3, T4 = [(tmp, tmp.h[:, a * 128:(a + 1) * 128]) for a in range(4)]
                    tt(P, "dve", T1, x0, cosb, ALU.mult)
                    tt(P, "dve", T2, x1, sinb, ALU.mult)
                    tt(P, "dve", T3, x1, cosb, ALU.mult)
                    tt(P, "dve", T4, x0, sinb, ALU.mult)
                    tt(P, "dve", x0, T1, T2, ALU.subtract)
                    tt(P, "dve", x1, T3, T4, ALU.add)
                    cp(P, "act", (ob, ob.h[:, 0:256]), (qk, qk.h[:, 0:256]))
                else:
                    cp(P, "act", (ob, ob.h[:, 0:256]), (pt, pt.h[:, 0:256]))
                dma(P, "sp", (d["qkv"], d["qkv"].h[gt * 128:(gt + 1) * 128, sl * 256:(sl + 1) * 256]), (ob, ob.h[:, 0:256]))

            self.linear_tok(lambda sl: (d["r_w"], d["r_w"].h[ri, sl]), 96, TB // 128, epi,
                            lambda kc, t: self.at_split(kc, t * 128, 128))
        if prev is not None:
            self.flip_x()

        cst, sm = s["cst"], s["small"]
        zt = s["z"]
        zv = [P.view(zt, zt.h[:, 0:4096], f"zA{li}"), P.view(zt, zt.h[:, 4096:8192], f"zB{li}")]
        P.split(zt, zv)
        z16 = [v.h[:, :].bitcast(BF16) for v in zv]
        m16 = s["misc16"]
        MV = [P.view(m16, m16.h[:, 0:768], f"m16A{li}"), P.view(m16, m16.h[:, 768:1536], f"m16B{li}")]
        P.split(m16, MV)
        ogp = s["misc16b"]
        OG = [P.view(ogp, ogp.h[:, 0:512], f"ogA{li}"), P.view(ogp, ogp.h[:, 512:1024], f"ogB{li}")]
        P.split(ogp, OG)
        SMV = [P.view(sm, sm.h[:, 128:144], f"smA{li}"), P.view(sm, sm.h[:, 144:160], f"smB{li}")]
        gngb = P.view(sm, sm.h[:, 0:64], f"gng{li}")
        P.split(sm, SMV + [gngb])
        dma(P, "sp", W(gngb), (d["r_gn"], d["r_gn"].h[ri]))
        dma(P, "sp", (cst, cst.h[:, 1024:1040]), W(d["r_zeta"]))
        dma(P, "sp", (cst, cst.h[:, 1040:1056]), W(d["r_dc"]))
        qk_ = d["qkv"]

        def load_chunk(hg, n):
            zb, zz = zv[n % 2], z16[n % 2]
            rows = slice(n * 128, (n + 1) * 128)
            dma(P, "sp", (zb, zz[:, 0:1024]), (qk_, qk_.h[rows, hg * 1024:(hg + 1) * 1024]))
            dma(P, "sp", (zb, zz[:, 1024:2048]), (qk_, qk_.h[rows, 4096 + hg * 1024:4096 + (hg + 1) * 1024]))
            dma(P, "sp", (zb, zz[:, 2048:4096]), (qk_, qk_.h[rows, 8192 + hg * 2048:8192 + (hg + 1) * 2048]))
            dma(P, "sp", (zb, zz[:, 4096:6144]), (qk_, qk_.h[rows, 16384 + hg * 2048:16384 + (hg + 1) * 2048]))

        for hg in range(4):
            dma(P, "sp", (cst, cst.h[:, 0:512].rearrange("p (h i) -> p h i", h=4)), (d["r_dt"], d["r_dt"].h[:, hg * 4:(hg + 1) * 4, :]))
            dma(P, "sp", (cst, cst.h[:, 512:1024].rearrange("p (h i) -> p h i", h=4)), (d["r_xi"], d["r_xi"].h[:, hg * 4:(hg + 1) * 4, :]))
            for sb_ in (s["stA"], s["stB"]):
                memset(P, "pool", W(sb_), 0.0)
            items = [(n, hl) for n in range(NT) for hl in range(4)]
            NI = len(items)

            def ctx(k):
                n, hl = items[k]
                par = k % 2
                zb, zz = zv[n % 2], z16[n % 2]
                return dict(n=n, hl=hl, h=hg * 4 + hl, par=par, zb=zb,
                            qh=zz[:, hl * 256:(hl + 1) * 256], kh=zz[:, 1024 + hl * 256:1024 + (hl + 1) * 256],
                            vh=(zb, zz[:, 2048 + hl * 512:2048 + (hl + 1) * 512]),
                            gh=(zb, zz[:, 4096 + hl * 512:4096 + (hl + 1) * 512]),
                            stf=s["stA"] if hl < 2 else s["stB"], stb=s["sbA"] if hl < 2 else s["sbB"], so=(hl % 2) * 1024,
                            M=MV[par], AT_=s["sb16"][par], kz=s["sb16"][2 + par], wv=s["sb16"][4 + par],
                            pS=self.ps[par], pO=self.ps[2 + par], on=s["st"][par], sg=s["st"][2 + par],
                            smv=SMV[par], og=OG[par])

            def S1(c):
                zb, M, hl, h = c["zb"], c["M"], c["hl"], c["h"]
                ptt = self.ps[6]
                p16 = ptt.h[:, :].bitcast(BF16)
                for dc in range(2):
                    tr(P, (ptt, p16[:, dc * 128:(dc + 1) * 128]), (zb, c["qh"][:, dc * 128:(dc + 1) * 128]), W(s["identb"]))
                    tr(P, (ptt, p16[:, 256 + dc * 128:256 + (dc + 1) * 128]), (zb, c["kh"][:, dc * 128:(dc + 1) * 128]), W(s["identb"]))
                cp(P, "act", (M, M.h[:, 0:256]), (ptt, p16[:, 0:256]))
                cp(P, "act", (M, M.h[:, 512:768]), (ptt, p16[:, 256:512]))
                tt(P, "dve", (M, M.h[:, 256:512].rearrange("p (c i) -> p c i", c=2)), (M, M.h[:, 0:256].rearrange("p (c i) -> p c i", c=2)),
                   (cst, cst.h[:, 512 + hl * 128:512 + (hl + 1) * 128].unsqueeze(1).to_broadcast([128, 2, 128])), ALU.mult)
                pS = c["pS"]
                for dc in range(2):
                    mm(P, (pS, pS.h[:, 0:128]), (M, M.h[:, 512 + dc * 128:512 + (dc + 1) * 128]), (M, M.h[:, dc * 128:(dc + 1) * 128]), dc == 0, dc == 1)
                AT_ = c["AT_"]
                tt(P, "dve", (AT_, AT_.h[:, 0:128]), (pS, pS.h[:, 0:128]), (cst, cst.h[:, hl * 128:(hl + 1) * 128]), ALU.mult)
                kz = c["kz"]
                ts(P, "pool", (kz, kz.h[:, 0:256]), (zb, c["kh"]), (cst, cst.h[:, 1024 + h:1025 + h]), ALU.mult)

            def S2(c):
                M, n, h, so, stf, stb = c["M"], c["n"], c["h"], c["so"], c["stf"], c["stb"]
                AT_, kz, pO, vh = c["AT_"], c["kz"], c["pO"], c["vh"]
                mm(P, (pO, pO.h[:, :]), (AT_, AT_.h[:, 0:128]), vh, True, n == 0)
                if n > 0:
                    for dc in range(2):
                        mm(P, (pO, pO.h[:, :]), (M, M.h[:, 256 + dc * 128:256 + (dc + 1) * 128]),
                           (stb, stb.h[:, so + dc * 512:so + (dc + 1) * 512]), False, dc == 1)
                for dc in range(2):
                    pD = self.ps[4 + dc]
                    mm(P, (pD, pD.h[:, :]), (kz, kz.h[:, dc * 128:(dc + 1) * 128]), vh, True, True)
                    sf = (stf, stf.h[:, so + dc * 512:so + (dc + 1) * 512])
                    stt(P, sf, sf, (cst, cst.h[:, 1040 + h:1041 + h]), (pD, pD.h[:, :]), ALU.mult, ALU.add)
                    cp(P, "act", (stb, stb.h[:, so + dc * 512:so + (dc + 1) * 512]), sf)
                smv = c["smv"]
                stats = (smv, smv.h[:, 0:6])
                mv = (smv, smv.h[:, 6:8])
                rs = (smv, smv.h[:, 8:9])
                nb = (smv, smv.h[:, 9:10])
                P.op("dve", lambda e, o=stats[1], i=pO.h[:, :]: e.bn_stats(o, i), r=[pO], w=[smv])
                P.op("dve", lambda e, o=mv[1], i=stats[1]: e.bn_aggr(o, i), r=[smv], w=[smv])
                ts(P, "dve", rs, (smv, smv.h[:, 7:8]), GN_EPS, ALU.add)
                act(P, rs, rs, AF.Sqrt)
                P.op("dve", lambda e, o=rs[1], i=rs[1]: e.reciprocal(o, i), r=[smv], w=[smv])
                stt(P, nb, (smv, smv.h[:, 6:7]), -1.0, rs, ALU.mult, ALU.mult)
                on, sg, wv = c["on"], c["sg"], c["wv"]
                act(P, W(on), (pO, pO.h[:, :]), AF.Identity, scale=rs, bias=nb)
                act(P, W(sg), c["gh"], AF.Silu)
                tt(P, "dve", (wv, wv.h[:, 0:512]), W(on), W(sg), ALU.mult)

            def S3(c):
                wv, og, h, n = c["wv"], c["og"], c["h"], c["n"]
                ptw = self.ps[7]
                w16 = ptw.h[:, :].bitcast(BF16)
                for cc in range(4):
                    tr(P, (ptw, w16[:, cc * 128:(cc + 1) * 128]), (wv, wv.h[:, cc * 128:(cc + 1) * 128]), W(s["identb"]))
                for cc in range(4):
                    act(P, (og, og.h[:, cc * 128:(cc + 1) * 128]), (ptw, w16[:, cc * 128:(cc + 1) * 128]), AF.Identity,
                        scale=(gngb, gngb.h[:, h * 4 + cc:h * 4 + cc + 1]))
                dst = d["oT"].h[h * 512:(h + 1) * 512, n * 128:(n + 1) * 128].rearrange("(c p) i -> p c i", p=128)
                dma(P, "sp", (d["oT"], dst), (og, og.h[:, 0:512].rearrange("p (c i) -> p c i", c=4)))

            load_chunk(hg, 0)
            if NT > 1:
                load_chunk(hg, 1)
            for t in range(NI + 2):
                if t < NI:
                    S1(ctx(t))
                if 1 <= t <= NI:
                    S2(ctx(t - 1))
                    n_done, hl_done = items[t - 1]
                    if hl_done == 3 and n_done + 2 < NT:
                        load_chunk(hg, n_done + 2)
                if t >= 2:
                    S3(ctx(t - 2))
        P.join(zt, zv)
        P.join(m16, MV)
        P.join(ogp, OG)
        P.join(sm, SMV + [gngb])

        TBM = self.TBM
        ATa, ATb = s["ATa"], s["ATb"]
        for blk in range(S // TBM):
            tok0 = blk * TBM
            for (ab, r0) in ((ATa, 0), (ATb, 4096)):
                src = d["oT"].h[r0:r0 + 4096, tok0:tok0 + TBM].rearrange("(k p) t -> p k t", p=128)
                dma(P, "sp", (ab, ab.h[:, 0:32 * TBM].rearrange("p (k t) -> p k t", k=32)), (d["oT"], src))

            def epi_o(cc, b, pt, tok0=tok0):
                ob = s["st"][2 + cc % 2]
                cp(P, "act", (ob, ob.h[:, 0:TBM]), (pt, pt.h[:, 0:TBM]))
                dma(P, "sp", (d["yT"], d["yT"].h[cc * 128:(cc + 1) * 128, tok0:tok0 + TBM]), (ob, ob.h[:, 0:TBM]))

            def rhs(kc, b):
                ab = ATa if kc < 32 else ATb
                return (ab, ab.h[:, (kc % 32) * TBM:(kc % 32 + 1) * TBM])

            self.linear_feat(lambda sl: (d["r_wo"], d["r_wo"].h[ri, sl]), 32, 128, 1, TBM, 64, epi_o, rhs)

    def moe(self, li, prev):
        P, s, d = self.P, self.s, self.d
        S = self.S
        TB = min(1024, S)
        BW = min(512, TB)
        NB = TB // BW
        self.setup_vectors(li, li, False, prev)
        cst = s["cst"]
        rw32 = s["stB"]
        dma(P, "sp", (rw32, rw32.h[:, 0:KC * NEXP].rearrange("p (k e) -> p k e", k=KC)), (d["m_rw"], d["m_rw"].h[li]))
        rbb = (cst, cst.h[:, 512:512 + NEXP])
        dma(P, "sp", rbb, (d["m_rb"], d["m_rb"].h[li].partition_broadcast(128)))
        bgu = (cst, cst.h[:, 1024:1024 + NEXP * 4])
        dma(P, "sp", bgu, (d["m_bgu"], d["m_bgu"].h[li]))
        bdl, bdh = s["sbA"], s["sbB"]
        dma(P, "pool", (bdl, bdl.h[0:NEXP, :]), (d["m_bdn"], d["m_bdn"].h[li, :, 0:2048]))
        dma(P, "pool", (bdh, bdh.h[0:NEXP, :]), (d["m_bdn"], d["m_bdn"].h[li, :, 2048:4096]))
        ntile = TB // 128
        zt = s["z"]
        z16 = zt.h[:, :].bitcast(BF16)
        sm = s["small"]
        for blk in range(S // TB):
            tok0 = blk * TB

            def router_done(b0, n):
                for tl in range(n // 128):
                    t = b0 // 128 + tl
                    pr = self.ps[6 + tl % 2]
                    tt(P, "dve", (cst, cst.h[:, t * NEXP:(t + 1) * NEXP]), (pr, pr.h[:, 0:NEXP]), rbb, ALU.add)

            self.prologue(lambda kc, b0, n: self.at_split(kc, b0, n) if TB > 512 else (s["ATa"], s["ATa"].h[:, kc * 512 + b0:kc * 512 + b0 + n]),
                          TB, tok0, TB, True, 0, 1, 2, 3, 4, 5, router=(rw32, [self.ps[6], self.ps[7]], router_done))
            cT = s["st"][5]
            cTb = s["sb16"][5]
            for t in range(ntile):
                lg = (cst, cst.h[:, t * NEXP:(t + 1) * NEXP])
                t8 = (sm, sm.h[:, 32:40])
                ex = (sm, sm.h[:, 40:72])
                mk = (sm, sm.h[:, 72:104])
                nm = (sm, sm.h[:, 104:105])
                sm_ = (sm, sm.h[:, 105:106])
                P.op("dve", lambda e, o=t8[1], i=lg[1]: e.max(o, i), r=[cst], w=[sm])
                ts(P, "dve", mk, lg, (sm, sm.h[:, 35:36]), ALU.is_ge)
                ts(P, "dve", nm, (sm, sm.h[:, 32:33]), -1.0, ALU.mult)
                act(P, ex, lg, AF.Exp, bias=nm)
                tt(P, "dve", ex, ex, mk, ALU.mult)
                P.op("dve", lambda e, o=sm_[1], i=ex[1]: e.reduce_sum(o, i, axis=mybir.AxisListType.X), r=[sm], w=[sm])
                P.op("dve", lambda e, o=sm_[1], i=sm_[1]: e.reciprocal(o, i), r=[sm], w=[sm])
                ts(P, "dve", ex, ex, sm_, ALU.mult)
                pt = self.ps[t % 2]
                tr(P, (pt, pt.h[0:NEXP, 0:128]), ex, W(s["ident"]))
                c4 = (t % 4) * 128
                cp(P, "dve", (cT, cT.h[0:NEXP, c4:c4 + 128]), (pt, pt.h[0:NEXP, 0:128]))
                cp(P, "dve", (cTb, cTb.h[0:NEXP, t * 128:(t + 1) * 128]), (cT, cT.h[0:NEXP, c4:c4 + 128]))
                if t % 4 == 3 or t == ntile - 1:
                    t0_ = (t // 4) * 512
                    n_ = (t % 4 + 1) * 128
                    dma(P, "sp", (d["cmbT"], d["cmbT"].h[:, t0_:t0_ + n_]), (cT, cT.h[0:NEXP, 0:n_]))
            for grp in range(4):
                def wfn(sl, grp=grp):
                    el, j = divmod(sl, 2)
                    return (d["m_wgu"], d["m_wgu"].h[li, grp * 8 + el, j])

                def epi_gu(cc, b, pt, grp=grp):
                    sl, hsel = divmod(cc, 2)
                    el, j = divmod(sl, 2)
                    e_ = grp * 8 + el
                    col = 1024 + e_ * 4 + hsel * 2 + j
                    bias = (cst, cst.h[:, col:col + 1])
                    g_ = (s["st"][b], s["st"][b].h[:, 0:BW])
                    if hsel == 0:
                        sg = (s["st"][2], s["st"][2].h[:, 0:BW])
                        ts(P, "dve", g_, (pt, pt.h[:, 0:BW]), bias, ALU.add, 7.0, ALU.min)
                        act(P, sg, g_, AF.Sigmoid, scale=1.702)
                        tt(P, "pool", g_, g_, sg, ALU.mult)
                        if j == 0:
                            cb = s["st"][4 + b]
                            dma(P, "sp", (cb, cb.h[:, 0:BW]),
                                (d["cmbT"], d["cmbT"].h[e_:e_ + 1, b * BW:(b + 1) * BW].partition_broadcast(128)))
                    else:
                        uu = (s["st"][3], s["st"][3].h[:, 0:BW])
                        cb = s["st"][4 + b]
                        ts(P, "dve", uu, (pt, pt.h[:, 0:BW]), bias, ALU.add, 7.0, ALU.min)
                        ts(P, "dve", uu, uu, -7.0, ALU.max, 1.0, ALU.add)
                        tt(P, "dve", uu, uu, g_, ALU.mult)
                        kcl = el * 2 + j
                        c0 = kcl * TB + b * BW
                        tt(P, "dve", (zt, z16[:, c0:c0 + BW]), uu, (cb, cb.h[:, 0:BW]), ALU.mult)

                self.linear_feat(wfn, 16, 256, NB, BW, KC, epi_gu,
                                 lambda kc, b: self.at_split(kc, b * 512, BW) if TB > 512 else (s["ATa"], s["ATa"].h[:, kc * 512:kc * 512 + BW]))

                def extra_dn(cc, b, pt):
                    bd_ = bdl if cc < 16 else bdh
                    co = (cc % 16) * 128
                    mm(P, (pt, pt.h[:, 0:BW]), (bd_, bd_.h[0:NEXP, co:co + 128]), (cTb, cTb.h[0:NEXP, b * BW:(b + 1) * BW]), False, True)

                def epi_dn(cc, b, pt, tok0=tok0, grp=grp):
                    ob = s["st"][cc % 2]
                    c0 = tok0 + b * BW
                    dst = (d["yT"], d["yT"].h[cc * 128:(cc + 1) * 128, c0:c0 + BW])
                    if grp == 0:
                        cp(P, "act", (ob, ob.h[:, 0:BW]), (pt, pt.h[:, 0:BW]))
                    else:
                        pb = s["st"][2 + cc % 2]
                        dma(P, "sp", (pb, pb.h[:, 0:BW]), dst)
                        tt(P, "dve", (ob, ob.h[:, 0:BW]), (pt, pt.h[:, 0:BW]), (pb, pb.h[:, 0:BW]), ALU.add)
                    dma(P, "sp", dst, (ob, ob.h[:, 0:BW]))

                self.linear_feat(lambda sl, grp=grp: (d["m_wdn"], d["m_wdn"].h[li, grp, sl]), 8, 512, NB, BW, 16, epi_dn,
                                 lambda kc, b: (zt, z16[:, kc * TB + b * BW:kc * TB + (b + 1) * BW]),
                                 extra=(extra_dn if grp == 0 else None))
        self.flip_x()

    def final_out(self, li):
        P, s, d = self.P, self.s, self.d
        S = self.S
        prev = dict(li=li, k=1, gate=5)
        mv, mb, ml = d["modv"], d["mod_b"], d["mod_layer"]
        self.load_vec_T([(0, [self.mv(5), (mb, mb.h[5]), (ml, ml.h[li, 5])], True),
                         (4, [(d["ln_gain"], d["ln_gain"].h[li, 1])], False),
                         (5, [(d["ln_bias"], d["ln_bias"].h[li, 1])], False)])
        memset(P, "dve", self.vec(1), 0.0)
        for tok0 in range(0, S, 256):
            self.prologue(None, 0, tok0, min(256, S), True, 0, 1, 2, 3, 4, 5)
        self.flip_x()
        xin = self.xT(self.xcur)
        ot = s["z"]
        for t in range(self.NT):
            half = (t % 2) * 4096
            for g in range(KC // 4):
                ib = s["st"][g % 4]
                src = xin.h[g * 512:(g + 1) * 512, t * 128:(t + 1) * 128].rearrange("(k p) q -> p k q", p=128)
                dma(P, "sp", (ib, ib.h[:, :].rearrange("p (k q) -> p k q", k=4)), (xin, src))
                pt = self.ps[g % 2]
                for j in range(4):
                    tr(P, (pt, pt.h[:, j * 128:(j + 1) * 128]), (ib, ib.h[:, j * 128:(j + 1) * 128]), W(s["ident"]))
                cp(P, "act" if g % 2 == 0 else "dve", (ot, ot.h[:, half + g * 512:half + (g + 1) * 512]), W(pt))
            tok = dma(P, "sp", (d["out"], d["out"].h[t * 128:(t + 1) * 128, :]), (ot, ot.h[:, half:half + 4096]))
            P.finish_on(tok)

    def build(self):
        self.load_consts()
        if self.do_mod:
            self.phase_mod()
        self.phase_rot()
        self.phase_xT()
        prev = None
        ai = ri = 0
        for n, li_real in enumerate(self.layers):
            li = n
            if li_real % 2 == 0:
                self.attention(li, ai, prev)
                bo = (self.d["a_bo"], self.d["a_bo"].h[ai])
                ai += 1
            else:
                self.retention(li, ri, prev)
                bo = None
                ri += 1
            self.moe(li, dict(li=li, k=0, gate=2, bo=bo))
            prev = dict(li=li, k=1, gate=5)
        self.final_out(len(self.layers) - 1)
        return self.P.build()


def _tile_w(w, ncols):
    K, N = w.shape
    return np.ascontiguousarray(w.reshape(K // 128, 128, N // ncols, ncols).transpose(2, 1, 0, 3))


def _kp(v):
    return np.ascontiguousarray(v.reshape(v.shape[:-1] + (KC, 128)))


def _consts(S):
    c = {}
    inv_a = np.float32(500000.0) ** (-(np.arange(0, 16, 2, dtype=np.float32) / np.float32(16)))
    inv_r = np.float32(10000.0) ** (-np.linspace(0.0, 1.0, 128, dtype=np.float32))
    inv = np.concatenate([inv_a, inv_r]).astype(np.float32)
    c["inv"] = np.ascontiguousarray(np.broadcast_to(inv[None, :], (128, 136)))
    j = np.arange(128)[:, None]
    i = np.arange(128)[None, :]
    mask = np.stack([(j > i), (j <= i)], axis=1).astype(np.float32)
    c["a_mask"] = np.ascontiguousarray(mask)
    c["ident"] = np.eye(128, dtype=np.float32)
    h = np.arange(16, dtype=np.float64)
    logd = np.log1p(-np.exp2(-5.0 - h))
    idx = np.arange(128, dtype=np.float64)
    rel = idx[None, :] - idx[:, None]
    dt = np.where(rel[None] >= 0, np.exp(np.maximum(rel, 0.0)[None] * logd[:, None, None]), 0.0) / 16.0
    c["r_dt"] = np.ascontiguousarray(dt.transpose(1, 0, 2)).astype(np.float32)
    xi = np.exp((idx + 1.0)[None, :] * logd[:, None])
    c["r_xi"] = np.ascontiguousarray(np.broadcast_to(xi[None], (128, 16, 128))).astype(np.float32)
    zeta = np.exp((127.0 - idx)[None, :] * logd[:, None]) / 16.0
    c["r_zeta"] = np.ascontiguousarray(zeta.T).astype(np.float32)
    dc = np.exp(128.0 * logd)
    c["r_dc"] = np.ascontiguousarray(np.broadcast_to(dc[None, :], (128, 16))).astype(np.float32)
    return c


def prep_shared(inp, layers, S):
    f = lambda a: np.asarray(a, dtype=np.float32)
    m = {}
    cst = _consts(S)
    m["inv"], m["ident"] = cst["inv"], cst["ident"]
    m["mod_w"] = _tile_w(f(inp["mod_w"]), 256)
    m["mod_b"] = f(inp["mod_b"]).reshape(6, KC, 128)
    m["mod_layer"] = _kp(f(inp["mod_layer"])[layers])
    m["ln_gain"] = _kp(f(inp["ln_gain"])[layers])
    m["ln_bias"] = _kp(f(inp["ln_bias"])[layers])
    al = [i // 2 for i in layers if i % 2 == 0]
    rl = [i // 2 for i in layers if i % 2 == 1]
    if al:
        m["a_mask"] = cst["a_mask"]
        m["a_wqkv"] = np.stack([_tile_w(f(inp["attn_w_qkv"][a]), 256) for a in al])
        m["a_bqkv"] = f(inp["attn_b_qkv"])[al].reshape(len(al), 1, QKV)
        sk = f(inp["attn_sinks"])[al]
        m["a_sink"] = np.stack([np.repeat(s_.reshape(32, 2).T, 64, axis=0) for s_ in sk])
        m["a_wo"] = np.stack([_tile_w(f(inp["attn_w_o"][a]), 256) for a in al])
        m["a_bo"] = _kp(f(inp["attn_b_o"])[al])
    if rl:
        for k in ("r_dt", "r_xi", "r_zeta", "r_dc"):
            m[k] = cst[k]
        m["r_w"] = np.stack([_tile_w(f(inp["ret_w_qkvg"][r]), 256) for r in rl])
        gn = f(inp["ret_gn_gain"])[rl]
        m["r_gn"] = np.ascontiguousarray(gn.reshape(len(rl), 64, 128).transpose(0, 2, 1))
        m["r_wo"] = np.stack([_tile_w(f(inp["ret_w_o"][r]), 128) for r in rl])
    m["m_rw"] = np.stack([np.ascontiguousarray(f(inp["router_w"][i]).reshape(KC, 128, NEXP).transpose(1, 0, 2)) for i in layers])
    m["m_rb"] = f(inp["router_b"])[layers].reshape(len(layers), 1, NEXP)
    m["m_wgu"] = np.stack([np.stack([
        np.ascontiguousarray(f(inp["expert_w_gu"][i][e]).reshape(KC, 128, 2, 2, 128).transpose(3, 1, 0, 2, 4).reshape(2, 128, KC, 256))
        for e in range(NEXP)]) for i in layers])
    m["m_bgu"] = np.stack([np.ascontiguousarray(f(inp["expert_b_gu"][i]).reshape(NEXP, 4, 128).transpose(2, 0, 1).reshape(128, NEXP * 4)) for i in layers])
    m["m_wdn"] = np.stack([np.stack([_tile_w(f(inp["expert_w_down"][i]).reshape(NEXP * EFF, D)[h * 2048:(h + 1) * 2048], 512)
                                     for h in range(4)]) for i in layers])
    m["m_bdn"] = f(inp["expert_b_down"])[layers]
    return m


def prep_core(inp, b, S):
    m = {}
    m["x"] = np.ascontiguousarray(np.asarray(inp["x"][b][:S], dtype=np.float32))
    m["cT"] = np.ascontiguousarray(np.asarray(inp["c"][b], dtype=np.float32).reshape(KC, 128).T)
    m["pos"] = np.ascontiguousarray(np.asarray(inp["positions"][b][:S], dtype=np.int32).reshape(S // 128, 128).T)
    return m


_CACHE = {}


def run_model(inp, S, layers, batches):
    key = (S, tuple(layers))
    if key not in _CACHE:
        _CACHE[key] = Model(S, list(layers)).build()
    nc = _CACHE[key]
    shared = prep_shared(inp, list(layers), S)
    in_maps = []
    for b in batches:
        mcore = dict(shared)
        mcore.update(prep_core(inp, b, S))
        in_maps.append(mcore)
    res = run_bass_kernel_spmd(nc, in_maps, core_ids=list(range(len(batches))))
    return np.stack([np.asarray(r["out"]) for r in res.results])


def kernel(**inputs):
    out = run_model(inputs, 4096, [0, 1, 2, 3], [0, 1])
    return out.astype(np.float32)
```

```python
import math
import numpy as np
from contextlib import ExitStack
import concourse.bass as bass
import concourse.mybir as mybir
from concourse.bass_utils import run_bass_kernel_spmd

F32 = mybir.dt.float32
BF16 = mybir.dt.bfloat16
I32 = mybir.dt.int32
ALU = mybir.AluOpType
AF = mybir.ActivationFunctionType

SAME_ENGINE_SYNC = True

D = 4096
KC = D // 128
DEPTH = 4
QKV = 5120
RPROJ = 24576
RV = 8192
NEXP = 32
EFF = 256
ALPHA = (2 * DEPTH) ** 0.25
LN_EPS = 1e-5
GN_EPS = 1e-6
TWO_PI = 2.0 * math.pi
CW1 = 6.28125
CW2 = TWO_PI - CW1


class Buf:
    def __init__(self, h, name, dram=False):
        self.h = h
        self.name = name
        self.dram = dram
        self.last_w = None
        self.readers = []
        self.dsem = None
        self.dcount = 0
        self.last_dma = None

    def __getitem__(self, idx):
        return self.h[idx]


class Prog:
    ENGS = ["pe", "act", "dve", "pool", "sp"]

    def __init__(self):
        self.nc = bass.Bass("TRN2", target_bir_lowering=False)
        self.es = ExitStack()
        self.streams = {e: [] for e in self.ENGS}
        self.final_tokens = []

    def dram(self, name, shape, dtype, kind="Internal"):
        t = self.nc.dram_tensor(name, list(shape), dtype, kind=kind)
        return Buf(t.ap(), name, dram=True)

    def sb(self, name, shape, dtype):
        h = self.es.enter_context(self.nc.sbuf_tensor(name, list(shape), dtype))
        return Buf(h, name)

    def ps(self, name, shape, dtype):
        h = self.es.enter_context(self.nc.psum_tensor(name, list(shape), dtype))
        b = Buf(h, name)
        b.psum = True
        return b

    def _deps(self, r, w):
        deps = []
        for b in r:
            if b.last_w is not None:
                deps.append(b.last_w)
            if getattr(b, "psum", False):
                deps.extend(b.readers)
        for b in w:
            if b.last_w is not None:
                deps.append(b.last_w)
            deps.extend(b.readers)
        return deps

    def _commit(self, tok, r, w):
        key = (tok[0], tok[1] if tok[0] == "eng" else id(tok[1]))
        for b in r:
            b.readers = [t for t in b.readers if (t[0], t[1] if t[0] == "eng" else id(t[1])) != key]
            b.readers.append(tok)
        for b in w:
            b.last_w = tok
            b.readers = []

    def op(self, eng, fn, r=(), w=()):
        deps = self._deps(r, w)
        idx = len(self.streams[eng])
        tok = ("eng", eng, idx)
        self.streams[eng].append(dict(kind="op", fn=fn, deps=deps, signal=False))
        self._commit(tok, r, w)
        return tok

    def dma(self, eng, out_ap, in_ap, r=(), w=(), owner=None):
        if owner is None:
            cands = [b for b in list(w) + list(r) if not b.dram]
            owner = cands[0] if cands else (list(w) + list(r))[0]
        deps = self._deps(r, w)
        if owner.last_dma is not None:
            deps.append(owner.last_dma)
        if owner.dsem is None:
            owner.dsem = self.es.enter_context(self.nc.semaphore("d_" + owner.name))
        owner.dcount += 16
        tok = ("dma", owner, owner.dcount)
        owner.last_dma = tok
        self.streams[eng].append(dict(kind="dma", out=out_ap, in_=in_ap, deps=deps, sem=owner.dsem))
        self._commit(tok, r, w)
        return tok

    def view(self, parent, ap, name):
        return Buf(ap, name)

    def split(self, parent, children):
        for c in children:
            c.last_w = parent.last_w
            c.readers = list(parent.readers)

    def join(self, parent, children):
        toks = list(parent.readers)
        for c in children:
            if c.last_w is not None:
                toks.append(c.last_w)
            toks.extend(c.readers)
        parent.readers = toks

    def finish_on(self, tok):
        self.final_tokens.append(tok)

    def build(self):
        nc = self.nc
        for e in self.ENGS:
            for o in self.streams[e]:
                for d in o["deps"]:
                    if d[0] == "eng":
                        self.streams[d[1]][d[2]]["signal"] = True
        counts = {}
        for e in self.ENGS:
            c = 0
            for i, o in enumerate(self.streams[e]):
                if o.get("signal"):
                    c += 1
                    counts[(e, i)] = c
        esem = {e: self.es.enter_context(nc.semaphore("e_" + e)) for e in self.ENGS}
        block = self.es.enter_context(nc.Block())
        engobj = {"pe": "tensor", "act": "scalar", "dve": "vector", "pool": "gpsimd", "sp": "sync"}
        if self.final_tokens:
            self.streams["sp"].append(dict(kind="wait", deps=list(self.final_tokens)))

        def emit_stream(e, engine):
            known = {}
            for o in self.streams[e]:
                for d in o["deps"]:
                    if d[0] == "eng":
                        if d[1] == e and (e == "pe" or not SAME_ENGINE_SYNC):
                            continue
                        key = ("eng", d[1])
                        val = counts[(d[1], d[2])]
                        sem = esem[d[1]]
                    else:
                        key = ("dma", id(d[1]))
                        val = d[2]
                        sem = d[1].dsem
                    if known.get(key, 0) >= val:
                        continue
                    known[key] = val
                    engine.wait_ge(sem, val)
                if o["kind"] == "op":
                    ins = o["fn"](engine)
                    if o["signal"]:
                        ins.then_inc(esem[e], 1)
                elif o["kind"] == "dma":
                    engine.dma_start(out=o["out"], in_=o["in_"]).then_inc(o["sem"], 16)

        for e in self.ENGS:
            if self.streams[e]:
                getattr(block, engobj[e])(lambda engine, e=e: emit_stream(e, engine))
        self.es.close()
        return nc


def _bufs(*ops):
    return [o[0] for o in ops if isinstance(o, tuple)]


def _ap(o):
    return o[1] if isinstance(o, tuple) else o


def mm(P, out, lhsT, rhs, start, stop):
    o, l, r_ = out[1], lhsT[1], rhs[1]
    return P.op("pe", lambda e: e.matmul(o, l, r_, start=start, stop=stop), r=[lhsT[0], rhs[0]], w=[out[0]])


def tr(P, out, in_, ident):
    o, i, d = out[1], in_[1], ident[1]
    return P.op("pe", lambda e: e.transpose(o, i, d), r=[in_[0], ident[0]], w=[out[0]])


def act(P, out, in_, func, scale=1.0, bias=0.0):
    o, i, s, b = out[1], in_[1], _ap(scale), _ap(bias)
    return P.op("act", lambda e: e.activation(o, i, func, bias=b, scale=s),
                r=[in_[0]] + _bufs(scale, bias), w=[out[0]])


def tt(P, eng, out, in0, in1, op):
    o, a, b = out[1], in0[1], in1[1]
    return P.op(eng, lambda e: e.tensor_tensor(o, a, b, op), r=[in0[0], in1[0]], w=[out[0]])


def ts(P, eng, out, in0, s1, op0, s2=None, op1=None):
    o, a, x1, x2 = out[1], in0[1], _ap(s1), _ap(s2)
    if op1 is None:
        return P.op(eng, lambda e: e.tensor_scalar(o, a, x1, None, op0), r=[in0[0]] + _bufs(s1), w=[out[0]])
    return P.op(eng, lambda e: e.tensor_scalar(o, a, x1, x2, op0, op1), r=[in0[0]] + _bufs(s1, s2), w=[out[0]])


def stt(P, out, in0, scalar, in1, op0, op1):
    o, a, s, b = out[1], in0[1], _ap(scalar), in1[1]
    return P.op("dve", lambda e: e.scalar_tensor_tensor(o, a, s, b, op0, op1),
                r=[in0[0], in1[0]] + _bufs(scalar), w=[out[0]])


def cp(P, eng, out, in_):
    o, i = out[1], in_[1]
    if eng == "act":
        return P.op("act", lambda e: e.copy(o, i), r=[in_[0]], w=[out[0]])
    return P.op(eng, lambda e: e.tensor_copy(o, i), r=[in_[0]], w=[out[0]])


def memset(P, eng, out, val):
    o = out[1]
    return P.op(eng, lambda e: e.memset(o, val), r=[], w=[out[0]])


def dma(P, eng, out, in_):
    return P.dma(eng, out[1], in_[1], r=[in_[0]], w=[out[0]])


def W(buf, ap=None):
    return (buf, buf.h[:] if ap is None else ap)


class Model:
    def __init__(self, S, layers, do_mod=True, final=True):
        self.S = S
        self.layers = layers
        self.NT = S // 128
        self.TBL = min(1024, S)
        self.TBM = min(512, S)
        self.LNB = 256
        self.P = Prog()
        self.do_mod = do_mod
        self.final = final
        self.alloc()

    def alloc(self):
        P, S = self.P, self.S
        nl = len(self.layers)
        n_attn = sum(1 for i in self.layers if i % 2 == 0)
        n_ret = sum(1 for i in self.layers if i % 2 == 1)
        self.n_attn, self.n_ret = n_attn, n_ret
        EI = "ExternalInput"
        d = {}
        d["x"] = P.dram("x", [S, D], F32, EI)
        d["cT"] = P.dram("cT", [128, KC], F32, EI)
        d["pos"] = P.dram("pos", [128, self.NT], I32, EI)
        d["inv"] = P.dram("inv", [128, 136], F32, EI)
        d["mod_w"] = P.dram("mod_w", [96, 128, KC, 256], F32, EI)
        d["mod_b"] = P.dram("mod_b", [6, KC, 128], F32, EI)
        d["mod_layer"] = P.dram("mod_layer", [nl, 6, KC, 128], F32, EI)
        d["ln_gain"] = P.dram("ln_gain", [nl, 2, KC, 128], F32, EI)
        d["ln_bias"] = P.dram("ln_bias", [nl, 2, KC, 128], F32, EI)
        if n_attn:
            d["a_wqkv"] = P.dram("a_wqkv", [n_attn, 20, 128, KC, 256], F32, EI)
            d["a_bqkv"] = P.dram("a_bqkv", [n_attn, 1, QKV], F32, EI)
            d["a_sink"] = P.dram("a_sink", [n_attn, 128, KC], F32, EI)
            d["a_wo"] = P.dram("a_wo", [n_attn, 16, 128, KC, 256], F32, EI)
            d["a_bo"] = P.dram("a_bo", [n_attn, KC, 128], F32, EI)
            d["a_mask"] = P.dram("a_mask", [128, 2, 128], F32, EI)
        if n_ret:
            d["r_w"] = P.dram("r_w", [n_ret, 96, 128, KC, 256], F32, EI)
            d["r_gn"] = P.dram("r_gn", [n_ret, 128, 64], F32, EI)
            d["r_wo"] = P.dram("r_wo", [n_ret, 32, 128, 64, 128], F32, EI)
            d["r_dt"] = P.dram("r_dt", [128, 16, 128], F32, EI)
            d["r_xi"] = P.dram("r_xi", [128, 16, 128], F32, EI)
            d["r_zeta"] = P.dram("r_zeta", [128, 16], F32, EI)
            d["r_dc"] = P.dram("r_dc", [128, 16], F32, EI)
        d["m_rw"] = P.dram("m_rw", [nl, 128, KC, NEXP], F32, EI)
        d["m_rb"] = P.dram("m_rb", [nl, 1, NEXP], F32, EI)
        d["m_wgu"] = P.dram("m_wgu", [nl, NEXP, 2, 128, KC, 256], F32, EI)
        d["m_bgu"] = P.dram("m_bgu", [nl, 128, NEXP * 4], F32, EI)
        d["m_wdn"] = P.dram("m_wdn", [nl, 4, 8, 128, 16, 512], F32, EI)
        d["m_bdn"] = P.dram("m_bdn", [nl, NEXP, D], F32, EI)
        d["ident"] = P.dram("ident", [128, 128], F32, EI)
        d["out"] = P.dram("out", [S, D], F32, "ExternalOutput")
        d["xT0"] = P.dram("xT0", [D, S], F32)
        d["xT1"] = P.dram("xT1", [D, S], F32)
        d["yT"] = P.dram("yT", [D, S], F32)
        d["modv"] = P.dram("modv", [6, D], F32)
        d["rot"] = P.dram("rot", [S, 272], F32)
        d["qkv"] = P.dram("qkv", [S, RPROJ if n_ret else QKV], BF16)
        d["oT"] = P.dram("oT", [RV, S], BF16)
        d["cmbT"] = P.dram("cmbT", [NEXP, min(1024, S)], F32)
        self.d = d

        s = {}
        s["ATa"] = P.sb("ATa", [128, 16384], BF16)
        s["ATb"] = P.sb("ATb", [128, 16384], BF16)
        s["ws0"] = P.sb("ws0", [128, 8192], BF16)
        s["ws1"] = P.sb("ws1", [128, 8192], BF16)
        s["z"] = P.sb("z", [128, 8192], F32)
        s["st"] = [P.sb(f"st{i}", [128, 512], F32) for i in range(6)]
        s["sb16"] = [P.sb(f"sb16_{i}", [128, 1024], BF16) for i in range(6)]
        s["ident"] = P.sb("ident_s", [128, 128], F32)
        s["identb"] = P.sb("identb", [128, 128], BF16)
        s["ones"] = P.sb("ones_s", [128, 128], F32)
        s["vec"] = P.sb("vec", [128, 16, KC], F32)
        s["vraw"] = P.sb("vraw", [KC, 4, 128], F32)
        s["small"] = P.sb("small", [128, 1024], F32)
        s["rotA"] = P.sb("rotA", [128, self.NT, 16], F32)
        s["misc16"] = P.sb("misc16", [128, 4096], BF16)
        s["misc16b"] = P.sb("misc16b", [128, 3328], BF16)
        s["stA"] = P.sb("stA", [128, 2048], F32)
        s["stB"] = P.sb("stB", [128, 2048], F32)
        s["sbA"] = P.sb("sbA", [128, 2048], BF16)
        s["sbB"] = P.sb("sbB", [128, 2048], BF16)
        s["cst"] = P.sb("cst", [128, 1280], F32)
        self.s = s
        self.ps = [P.ps(f"ps{i}", [128, 512], F32) for i in range(8)]
        self.xcur = 0

    def ws(self, i):
        return self.s["ws0"] if i % 2 == 0 else self.s["ws1"]

    def load_consts(self):
        P, s, d = self.P, self.s, self.d
        dma(P, "sp", W(s["ident"]), W(d["ident"]))
        cp(P, "dve", W(s["identb"]), W(s["ident"]))
        memset(P, "dve", W(s["ones"]), 1.0)

    def vec(self, slot, kc=None):
        v = self.s["vec"]
        if kc is None:
            return (v, v.h[:, slot, :])
        return (v, v.h[:, slot, kc:kc + 1])

    def load_vec_T(self, slot_list):
        P, s = self.P, self.s
        vraw = s["vraw"]
        for (slot, terms, add_one) in slot_list:
            for j, (db, ap) in enumerate(terms):
                dma(P, "sp", (vraw, vraw.h[:, j, :]), (db, ap))
            acc = (vraw, vraw.h[:, 0, :])
            for j in range(1, len(terms)):
                tt(P, "dve", acc, acc, (vraw, vraw.h[:, j, :]), ALU.add)
            if add_one:
                ts(P, "dve", acc, acc, 1.0, ALU.add)
            pt = self.ps[7]
            tr(P, (pt, pt.h[:, 0:KC]), acc, (s["ident"], s["ident"].h[0:KC, 0:KC]))
            cp(P, "dve", self.vec(slot), (pt, pt.h[:, 0:KC]))

    def phase_mod(self):
        P, s, d = self.P, self.s, self.d
        sm = s["small"]
        cs = (sm, sm.h[:, 0:KC])
        dma(P, "sp", cs, W(d["cT"]))
        csb = (s["misc16"], s["misc16"].h[:, 0:KC])
        act(P, csb, cs, AF.Silu)
        orow = s["st"][0]
        nsl = 96
        dma(P, "pool", (self.ws(0), self.ws(0).h[:, :].rearrange("p (k n) -> p k n", k=KC)), (d["mod_w"], d["mod_w"].h[0]))
        for sl in range(nsl):
            if sl + 1 < nsl:
                wn = self.ws(sl + 1)
                dma(P, "pool", (wn, wn.h[:, :].rearrange("p (k n) -> p k n", k=KC)), (d["mod_w"], d["mod_w"].h[sl + 1]))
            wsl = self.ws(sl)
            pt = self.ps[sl % 2]
            for kc in range(KC):
                mm(P, (pt, pt.h[0:1, 0:256]), (csb[0], s["misc16"].h[:, kc:kc + 1]),
                   (wsl, wsl.h[:, kc * 256:(kc + 1) * 256]), kc == 0, kc == KC - 1)
            ob = s["st"][sl % 2]
            cp(P, "act", (ob, ob.h[0:1, 0:256]), (pt, pt.h[0:1, 0:256]))
            j, r = divmod(sl * 256, D)
            dma(P, "sp", (d["modv"], d["modv"].h[j:j + 1, r:r + 256]), (ob, ob.h[0:1, 0:256]))

    def phase_rot(self):
        P, s, d = self.P, self.s, self.d
        sm = s["small"]
        posi = s["cst"]
        pos_i = (posi, posi.h[:, 0:self.NT].bitcast(I32))
        dma(P, "sp", pos_i, W(d["pos"]))
        posf = (sm, sm.h[:, 64:64 + self.NT])
        cp(P, "dve", posf, pos_i)
        inv = (s["z"], s["z"].h[:, 0:136])
        dma(P, "sp", inv, W(d["inv"]))
        for t in range(self.NT):
            base = 256 + (t % 2) * 1024
            zz = s["z"]
            ang = (zz, zz.h[:, base:base + 136])
            kf = (zz, zz.h[:, base + 136:base + 272])
            ki = (zz, zz.h[:, base + 272:base + 408].bitcast(I32))
            r_ = (zz, zz.h[:, base + 408:base + 544])
            m_ = (zz, zz.h[:, base + 544:base + 680])
            rc = (zz, zz.h[:, base + 680:base + 816])
            ot = s["st"][t % 2]
            ts(P, "dve", ang, inv, (sm, sm.h[:, 64 + t:65 + t]), ALU.mult)
            ts(P, "dve", kf, ang, 1.0 / TWO_PI, ALU.mult)
            cp(P, "dve", ki, kf)
            cp(P, "dve", kf, ki)
            stt(P, r_, kf, -CW1, ang, ALU.mult, ALU.add)
            stt(P, r_, kf, -CW2, r_, ALU.mult, ALU.add)
            ts(P, "dve", m_, r_, math.pi, ALU.is_gt)
            stt(P, r_, m_, -TWO_PI, r_, ALU.mult, ALU.add)
            ts(P, "dve", m_, r_, -math.pi, ALU.is_lt)
            stt(P, r_, m_, TWO_PI, r_, ALU.mult, ALU.add)
            ts(P, "dve", rc, r_, math.pi / 2, ALU.add)
            ts(P, "dve", m_, rc, math.pi, ALU.is_gt)
            stt(P, rc, m_, -TWO_PI, rc, ALU.mult, ALU.add)
            ts(P, "dve", r_, r_, math.pi, ALU.min, -math.pi, ALU.max)
            ts(P, "dve", rc, rc, math.pi, ALU.min, -math.pi, ALU.max)
            act(P, (ot, ot.h[:, 0:8]), (zz, rc[1][:, 0:8]), AF.Sin)
            act(P, (ot, ot.h[:, 8:16]), (zz, r_[1][:, 0:8]), AF.Sin)
            act(P, (ot, ot.h[:, 16:144]), (zz, rc[1][:, 8:136]), AF.Sin)
            act(P, (ot, ot.h[:, 144:272]), (zz, r_[1][:, 8:136]), AF.Sin)
            dma(P, "sp", (d["rot"], d["rot"].h[t * 128:(t + 1) * 128, :]), (ot, ot.h[:, 0:272]))
        rotA = s["rotA"]
        dma(P, "sp", W(rotA), (d["rot"], d["rot"].h[:, 0:16].rearrange("(t p) c -> p t c", p=128)))

    def phase_xT(self):
        P, s, d = self.P, self.s, self.d
        xt = s["z"]
        for t in range(self.NT):
            half = (t % 2) * 4096
            xtile = (xt, xt.h[:, half:half + 4096])
            dma(P, "sp", xtile, (d["x"], d["x"].h[t * 128:(t + 1) * 128, :]))
            for g in range(KC // 4):
                pt = self.ps[g % 2]
                for j in range(4):
                    kc = g * 4 + j
                    tr(P, (pt, pt.h[:, j * 128:(j + 1) * 128]), (xt, xt.h[:, half + kc * 128:half + (kc + 1) * 128]), W(s["ident"]))
                ob = s["st"][g % 4]
                cp(P, "act" if g % 2 == 0 else "dve", W(ob), W(pt))
                dst = d["xT0"].h[g * 512:(g + 1) * 512, t * 128:(t + 1) * 128].rearrange("(k p) q -> p k q", p=128)
                dma(P, "sp", (d["xT0"], dst), (ob, ob.h[:, :].rearrange("p (k q) -> p k q", k=4)))
        self.xcur = 0

    def xT(self, which):
        return self.d["xT0"] if which == 0 else self.d["xT1"]

    def prologue(self, dst, dst_stride, tok0, ntok, ln, sl_g, sl_gb, sl_A, sl_B, sl_gain, sl_bias,
                 router=None, write_x=True):
        P, s, d = self.P, self.s, self.d
        LNB = min(self.LNB, ntok)
        xin = self.xT(self.xcur)
        xout = self.xT(1 - self.xcur)
        z = s["z"]
        dstfn = dst
        for b0 in range(0, ntok, LNB):
            c0 = tok0 + b0
            if ln:
                ps_s, ps_q = self.ps[4], self.ps[5]
                for kc in range(KC):
                    xs = s["st"][kc % 2]
                    ys = s["st"][2 + kc % 2]
                    sq = s["st"][4 + kc % 2]
                    dma(P, "sp", (xs, xs.h[:, 0:LNB]), (xin, xin.h[kc * 128:(kc + 1) * 128, c0:c0 + LNB]))
                    dma(P, "sp", (ys, ys.h[:, 0:LNB]), (d["yT"], d["yT"].h[kc * 128:(kc + 1) * 128, c0:c0 + LNB]))
                    act(P, (ys, ys.h[:, 0:LNB]), (ys, ys.h[:, 0:LNB]), AF.Identity, scale=self.vec(sl_g, kc), bias=self.vec(sl_gb, kc))
                    zk = (z, z.h[:, kc * LNB:(kc + 1) * LNB])
                    stt(P, zk, (xs, xs.h[:, 0:LNB]), ALPHA, (ys, ys.h[:, 0:LNB]), ALU.mult, ALU.add)
                    tt(P, "pool" if kc % 2 == 0 else "dve", (sq, sq.h[:, 0:LNB]), zk, zk, ALU.mult)
                    mm(P, (ps_s, ps_s.h[:, 0:LNB]), W(s["ones"]), zk, kc == 0, kc == KC - 1)
                    mm(P, (ps_q, ps_q.h[:, 0:LNB]), W(s["ones"]), (sq, sq.h[:, 0:LNB]), kc == 0, kc == KC - 1)
                sm = s["small"]
                mean = (sm, sm.h[:, 128:128 + LNB])
                rstd = (sm, sm.h[:, 384:384 + LNB])
                tmp = (sm, sm.h[:, 640:640 + LNB])
                ts(P, "dve", mean, (ps_s, ps_s.h[:, 0:LNB]), 1.0 / D, ALU.mult)
                ts(P, "dve", rstd, (ps_q, ps_q.h[:, 0:LNB]), 1.0 / D, ALU.mult)
                tt(P, "dve", tmp, mean, mean, ALU.mult)
                tt(P, "dve", rstd, rstd, tmp, ALU.subtract)
                ts(P, "dve", rstd, rstd, 0.0, ALU.max, LN_EPS, ALU.add)
                act(P, rstd, rstd, AF.Sqrt)
                P.op("dve", lambda e, o=rstd[1], i=rstd[1]: e.reciprocal(o, i), r=[sm], w=[sm])
            for kc in range(KC):
                if ln:
                    zk = (z, z.h[:, kc * LNB:(kc + 1) * LNB])
                    tt(P, "dve", zk, zk, mean, ALU.subtract)
                    tt(P, "dve", zk, zk, rstd, ALU.mult)
                    if write_x:
                        xo = s["st"][kc % 2]
                        ts(P, "pool", (xo, xo.h[:, 0:LNB]), zk, self.vec(sl_gain, kc), ALU.mult, self.vec(sl_bias, kc), ALU.add)
                        dma(P, "sp", (xout, xout.h[kc * 128:(kc + 1) * 128, c0:c0 + LNB]), (xo, xo.h[:, 0:LNB]))
                    src = zk
                else:
                    xs = s["st"][kc % 2]
                    dma(P, "sp", (xs, xs.h[:, 0:LNB]), (xin, xin.h[kc * 128:(kc + 1) * 128, c0:c0 + LNB]))
                    src = (xs, xs.h[:, 0:LNB])
                if dstfn is not None:
                    dd = dstfn(kc, b0, LNB)
                    if router is None:
                        act(P, dd, src, AF.Identity, scale=self.vec(sl_A, kc), bias=self.vec(sl_B, kc))
                    else:
                        uf = s["st"][2 + kc % 2]
                        act(P, (uf, uf.h[:, 0:LNB]), src, AF.Identity, scale=self.vec(sl_A, kc), bias=self.vec(sl_B, kc))
                        cp(P, "dve", dd, (uf, uf.h[:, 0:LNB]))
                        rw32, ps_list, _cb = router
                        for tl in range(LNB // 128):
                            pr = ps_list[tl % 2]
                            mm(P, (pr, pr.h[:, 0:NEXP]), (uf, uf.h[:, tl * 128:(tl + 1) * 128]),
                               (rw32, rw32.h[:, kc * NEXP:(kc + 1) * NEXP]), kc == 0, kc == KC - 1)
            if router is not None:
                router[2](b0, LNB)
        return

    def flip_x(self):
        self.xcur = 1 - self.xcur

    def mv(self, j):
        m = self.d["modv"]
        return (m, m.h[j:j + 1, :].rearrange("o (k p) -> (o k) p", p=128))

    def setup_vectors(self, li, i, first_sublayer, prev):
        P, s, d = self.P, self.s, self.d
        j0 = 0 if first_sublayer else 3
        mv, mb, ml = d["modv"], d["mod_b"], d["mod_layer"]
        items = []
        items.append((6, [self.mv(j0 + 1), (mb, mb.h[j0 + 1]), (ml, ml.h[li, j0 + 1])], True))
        items.append((7, [self.mv(j0), (mb, mb.h[j0]), (ml, ml.h[li, j0])], False))
        if prev is not None:
            pli, pk, pj = prev["li"], prev["k"], prev["gate"]
            items.append((0, [self.mv(pj), (mb, mb.h[pj]), (ml, ml.h[pli, pj])], True))
            items.append((4, [(d["ln_gain"], d["ln_gain"].h[pli, pk])], False))
            items.append((5, [(d["ln_bias"], d["ln_bias"].h[pli, pk])], False))
            if prev.get("bo") is not None:
                items.append((8, [prev["bo"]], False))
        self.load_vec_T(items)
        if prev is not None:
            if prev.get("bo") is not None:
                tt(P, "dve", self.vec(1), self.vec(0), self.vec(8), ALU.mult)
            else:
                memset(P, "dve", self.vec(1), 0.0)
            tt(P, "dve", self.vec(2), self.vec(4), self.vec(6), ALU.mult)
            tt(P, "dve", self.vec(3), self.vec(5), self.vec(6), ALU.mult)
            tt(P, "dve", self.vec(3), self.vec(3), self.vec(7), ALU.add)
        else:
            cp(P, "dve", self.vec(2), self.vec(6))
            cp(P, "dve", self.vec(3), self.vec(7))

    def at_split(self, kc, tok, n):
        b = self.s["ATa"] if tok < 512 else self.s["ATb"]
        c = kc * 512 + (tok % 512)
        return (b, b.h[:, c:c + n])

    def linear_tok(self, w_dram_fn, n_slabs, ntile, epilogue, lhs_fn, kch=KC, extra=None):
        P, s = self.P, self.s
        dma(P, "pool", (self.ws(0), self.ws(0).h[:, 0:kch * 256].rearrange("p (k n) -> p k n", k=kch)), w_dram_fn(0))
        for sl in range(n_slabs):
            if sl + 1 < n_slabs:
                wn = self.ws(sl + 1)
                dma(P, "pool", (wn, wn.h[:, 0:kch * 256].rearrange("p (k n) -> p k n", k=kch)), w_dram_fn(sl + 1))
            wsl = self.ws(sl)
            for t in range(ntile):
                pt = self.ps[t % 4]
                for kc in range(kch):
                    last = (kc == kch - 1) and extra is None
                    mm(P, (pt, pt.h[:, 0:256]), lhs_fn(kc, t), (wsl, wsl.h[:, kc * 256:(kc + 1) * 256]), kc == 0, last)
                if extra is not None:
                    extra(sl, t, pt)
                epilogue(sl, t, pt)

    def linear_feat(self, w_dram_fn, n_slabs, slab_cols, nblk, blk, kch, epilogue, rhs_fn, extra=None):
        P, s = self.P, self.s
        ncs = slab_cols // 128
        dma(P, "pool", (self.ws(0), self.ws(0).h[:, 0:kch * slab_cols].rearrange("p (k n) -> p k n", k=kch)), w_dram_fn(0))
        cnt = 0
        for sl in range(n_slabs):
            if sl + 1 < n_slabs:
                wn = self.ws(sl + 1)
                dma(P, "pool", (wn, wn.h[:, 0:kch * slab_cols].rearrange("p (k n) -> p k n", k=kch)), w_dram_fn(sl + 1))
            wsl = self.ws(sl)
            for c in range(ncs):
                for b in range(nblk):
                    pt = self.ps[cnt % 4]
                    cnt += 1
                    for kc in range(kch):
                        last = (kc == kch - 1) and extra is None
                        mm(P, (pt, pt.h[:, 0:blk]), (wsl, wsl.h[:, kc * slab_cols + c * 128:kc * slab_cols + (c + 1) * 128]),
                           rhs_fn(kc, b), kc == 0, last)
                    if extra is not None:
                        extra(sl * ncs + c, b, pt)
                    epilogue(sl * ncs + c, b, pt)

    def attention(self, li, ai, prev):
        P, s, d = self.P, self.s, self.d
        S, NT, TB = self.S, self.NT, self.TBL
        self.setup_vectors(li, li, True, prev)
        brow = (s["misc16b"], s["misc16b"].h[0:1, 0:QKV]) if False else None
        bqa, bqb = s["sbA"], s["sbB"]
        dma(P, "pool", (bqa, bqa.h[0:1, 0:2048]), (d["a_bqkv"], d["a_bqkv"].h[ai, :, 0:2048]))
        dma(P, "pool", (bqb, bqb.h[0:1, 0:2048]), (d["a_bqkv"], d["a_bqkv"].h[ai, :, 2048:4096]))
        bq2 = s["misc16b"]
        dma(P, "pool", (bq2, bq2.h[0:1, 0:1024]), (d["a_bqkv"], d["a_bqkv"].h[ai, :, 4096:QKV]))
        onesb = (s["misc16b"], s["misc16b"].h[0:1, 1024:1152])
        memset(P, "dve", onesb, 1.0)
        rotA = s["rotA"]

        for blk in range(S // TB):
            tok0 = blk * TB
            self.prologue(lambda kc, b0, n: self.at_split(kc, b0, n), TB, tok0, TB, prev is not None, 0, 1, 2, 3, 4, 5)
            if False:
                pass

            def extra(sl, t, pt):
                c0 = sl * 256
                if c0 < 2048:
                    rhs = (bqa, bqa.h[0:1, c0:c0 + 256])
                elif c0 < 4096:
                    rhs = (bqb, bqb.h[0:1, c0 - 2048:c0 - 2048 + 256])
                else:
                    rhs = (bq2, bq2.h[0:1, c0 - 4096:c0 - 4096 + 256])
                mm(P, (pt, pt.h[:, 0:256]), onesb, rhs, False, True)

            def epi(sl, t, pt, tok0=tok0):
                gt = tok0 // 128 + t
                qk = s["st"][t % 2]
                ob = s["sb16"][t % 2]
                if sl < 18:
                    cp(P, "act", (qk, qk.h[:, 0:256]), (pt, pt.h[:, 0:256]))
                    v3 = qk.h[:, 0:256].rearrange("p (h e) -> p h e", h=4)
                    x1, x2 = (qk, v3[:, :, 0:8]), (qk, v3[:, :, 8:16])
                    cosb = (rotA, rotA.h[:, gt, 0:8].unsqueeze(1).to_broadcast([128, 4, 8]))
                    sinb = (rotA, rotA.h[:, gt, 8:16].unsqueeze(1).to_broadcast([128, 4, 8]))
                    tmp = s["st"][4 + t % 2]
                    t3 = tmp.h[:, 0:128].rearrange("p (a h e) -> p a h e", a=4, h=4)
                    T1, T2, T3, T4 = [(tmp, t3[:, a]) for a in range(4)]
                    tt(P, "dve", T1, x1, cosb, ALU.mult)
                    tt(P, "dve", T2, x2, sinb, ALU.mult)
                    tt(P, "dve", T3, x2, cosb, ALU.mult)
                    tt(P, "dve", T4, x1, sinb, ALU.mult)
                    tt(P, "dve", x1, T1, T2, ALU.subtract)
                    tt(P, "dve", x2, T3, T4, ALU.add)
                    cp(P, "act", (ob, ob.h[:, 0:256]), (qk, qk.h[:, 0:256]))
                else:
                    cp(P, "act", (ob, ob.h[:, 0:256]), (pt, pt.h[:, 0:256]))
                dma(P, "sp", (d["qkv"], d["qkv"].h[gt * 128:(gt + 1) * 128, sl * 256:(sl + 1) * 256]), (ob, ob.h[:, 0:256]))

            self.linear_tok(lambda sl: (d["a_wqkv"], d["a_wqkv"].h[ai, sl]), 20, TB // 128, epi,
                            lambda kc, t: self.at_split(kc, t * 128, 128), extra=extra)
        if prev is not None:
            self.flip_x()

        cst = s["cst"]
        maskf = (cst, cst.h[:, 0:256].rearrange("p (a q) -> p a q", a=2))
        dma(P, "sp", maskf, W(d["a_mask"]))
        maskb = s["sbA"]
        mb3 = maskb.h[:, 0:256].rearrange("p (a q) -> p a q", a=2)
        cp(P, "dve", (maskb, mb3), maskf)
        es = (cst, cst.h[:, 256:256 + KC])
        dma(P, "sp", es, (d["a_sink"], d["a_sink"].h[ai]))
        act(P, es, es, AF.Exp)
        op_ = s["sbA"]
        OPv = op_.h[:, 256:512].rearrange("p (a c) -> p a c", a=2)
        memset(P, "dve", (op_, OPv), 0.0)
        memset(P, "dve", (op_, OPv[:, 0, 0:64]), 1.0)
        memset(P, "dve", (op_, OPv[:, 1, 64:128]), 1.0)
        vp = s["stA"]
        vpb = vp.h[:, :].bitcast(BF16)
        VP = vpb.rearrange("p (s a g c) -> p s a g c", s=2, a=2, g=8)
        memset(P, "pool", (vp, vpb), 0.0)
        ktr = s["misc16b"]
        KT = ktr.h[:, 1280:1280 + 2048].rearrange("p (s g k) -> p s g k", s=2, g=8)
        kd = s["sb16"][4]
        qtile = s["z"]
        qt16 = qtile.h[:, :].bitcast(BF16)
        QT = s["misc16"]
        for blk in range(S // TB):
            for qi in range(TB // 128):
                gt = blk * (TB // 128) + qi
                slot = gt % 2
                qh = (gt % 2) * 8192
                qv = qt16[:, qh:qh + 4096]
                kv = qt16[:, qh + 4096:qh + 4608]
                vv = qt16[:, qh + 4608:qh + 5120]
                dma(P, "sp", (qtile, qt16[:, qh:qh + QKV]), (d["qkv"], d["qkv"].h[gt * 128:(gt + 1) * 128, 0:QKV]))
                kd4 = kd.h[:, 0:1024].rearrange("p (g a e) -> p g a e", g=8, a=2)
                cp(P, "pool", (kd, kd4), (qtile, kv.rearrange("p (g e) -> p g e", g=8).unsqueeze(2).to_broadcast([128, 8, 2, 64])))
                ptk = self.ps[6]
                ptk16 = ptk.h[:, :].bitcast(BF16)
                for g in range(8):
                    tr(P, (ptk, ptk16[:, g * 128:(g + 1) * 128]), (kd, kd.h[:, g * 128:(g + 1) * 128]), W(s["identb"]))
                cp(P, "act", (ktr, KT[:, slot]), (ptk, ptk16.rearrange("p (g k) -> p g k", g=8)))
                v3 = vv.rearrange("p (g e) -> p g e", g=8)
                cp(P, "pool", (vp, VP[:, slot, 0, :, 0:64]), (qtile, v3))
                cp(P, "pool", (vp, VP[:, slot, 1, :, 64:128]), (qtile, v3))
                for gq in range(4):
                    ptq = self.ps[7]
                    ptq16 = ptq.h[:, :].bitcast(BF16)
                    for j in range(8):
                        c = gq * 8 + j
                        tr(P, (ptq, ptq16[:, j * 128:(j + 1) * 128]), (qtile, qv[:, c * 128:(c + 1) * 128]), W(s["identb"]))
                    cp(P, "dve" if gq % 2 else "act", (QT, QT.h[:, gq * 1024:(gq + 1) * 1024]), (ptq, ptq16))
                kts = [(1 - slot, 0), (slot, 1)] if gt > 0 else [(slot, 1)]
                for g in range(8):
                    Es = []
                    cnt = 0
                    for pi in range(2):
                        for (ks, mi) in kts:
                            pS = self.ps[cnt % 4]
                            lo, hi = pi * 64, pi * 64 + 64
                            mm(P, (pS, pS.h[:, :]), (ktr, KT[lo:hi, ks, g, :]),
                               (QT, QT.h[lo:hi, 4 * g * 128:(4 * g + 4) * 128]), True, True)
                            E = s["sb16"][cnt % 4]
                            act(P, (E, E.h[:, 0:512]), (pS, pS.h[:, :]), AF.Exp, scale=0.125)
                            E3 = E.h[:, 0:512].rearrange("p (j q) -> p j q", j=4)
                            tt(P, "pool", (E, E3), (E, E3), (maskb, mb3[:, mi, :].unsqueeze(1).to_broadcast([128, 4, 128])), ALU.mult)
                            Es.append((E, pi, ks))
                            cnt += 1
                    pO, pD = self.ps[4], self.ps[5]
                    for n_, (E, pi, ks) in enumerate(Es):
                        mm(P, (pO, pO.h[:, :]), (vp, VP[:, ks, pi, g, :]), (E, E.h[:, 0:512]), n_ == 0, n_ == len(Es) - 1)
                    for n_, (E, pi, ks) in enumerate(Es):
                        mm(P, (pD, pD.h[:, :]), (op_, OPv[:, pi, :]), (E, E.h[:, 0:512]), n_ == 0, n_ == len(Es) - 1)
                    den = s["st"][g % 2]
                    d3 = den.h[:, 0:512].rearrange("p (j q) -> p j q", j=4)
                    tt(P, "dve", (den, d3), (pD, pD.h[:, :].rearrange("p (j q) -> p j q", j=4)),
                       (cst, cst.h[:, 256 + 4 * g:256 + 4 * g + 4].unsqueeze(2).to_broadcast([128, 4, 128])), ALU.add)
                    P.op("dve", lambda e, o=den.h[:, 0:512], i=den.h[:, 0:512]: e.reciprocal(o, i), r=[den], w=[den])
                    for j in range(4):
                        c = 4 * g + j
                        tt(P, "dve", self.at_split(c, qi * 128, 128),
                           (pO, pO.h[:, j * 128:(j + 1) * 128]), (den, den.h[:, j * 128:(j + 1) * 128]), ALU.mult)
            nb = TB // 512 if TB >= 512 else 1
            bw = min(512, TB)

            def epi_o(cc, b, pt, blk=blk, bw=bw):
                ob = s["st"][2 + cc % 2]
                cp(P, "act", (ob, ob.h[:, 0:bw]), (pt, pt.h[:, 0:bw]))
                c0 = blk * TB + b * bw
                dma(P, "sp", (d["yT"], d["yT"].h[cc * 128:(cc + 1) * 128, c0:c0 + bw]), (ob, ob.h[:, 0:bw]))

            self.linear_feat(lambda sl: (d["a_wo"], d["a_wo"].h[ai, sl]), 16, 256, nb, bw, KC, epi_o,
                             lambda kc, b, bw=bw: self.at_split(kc, b * 512, bw))

    def retention(self, li, ri, prev):
        P, s, d = self.P, self.s, self.d
        S, NT, TB = self.S, self.NT, self.TBL
        self.setup_vectors(li, li, True, prev)
        rotR = s["stB"]
        for blk in range(S // TB):
            tok0 = blk * TB
            self.prologue(lambda kc, b0, n: self.at_split(kc, b0, n), TB, tok0, TB, prev is not None, 0, 1, 2, 3, 4, 5)
            for t in range(TB // 128):
                gt = tok0 // 128 + t
                dma(P, "sp", (rotR, rotR.h[:, t * 256:(t + 1) * 256]), (d["rot"], d["rot"].h[gt * 128:(gt + 1) * 128, 16:272]))

            def epi(sl, t, pt, tok0=tok0):
                gt = tok0 // 128 + t
                ob = s["sb16"][t % 2]
                if sl < 32:
                    qk = s["st"][t % 2]
                    cp(P, "act", (qk, qk.h[:, 0:256]), (pt, pt.h[:, 0:256]))
                    v3 = qk.h[:, 0:256].rearrange("p (e two) -> p e two", two=2)
                    x0, x1 = (qk, v3[:, :, 0]), (qk, v3[:, :, 1])
                    cosb = (rotR, rotR.h[:, t * 256:t * 256 + 128])
                    sinb = (rotR, rotR.h[:, t * 256 + 128:t * 256 + 256])
                    tmp = s["st"][4 + t % 2]
                    T1, T2, T3, T4 = [(tmp, tmp.h[:, a * 128:(a + 1) * 128]) for a in range(4)]
                    tt(P, "dve", T1, x0, cosb, ALU.mult)
                    tt(P, "dve", T2, x1, sinb, ALU.mult)
                    tt(P, "dve", T3, x1, cosb, ALU.mult)
                    tt(P, "dve", T4, x0, sinb, ALU.mult)
                    tt(P, "dve", x0, T1, T2, ALU.subtract)
                    tt(P, "dve", x1, T3, T4, ALU.add)
                    cp(P, "act", (ob, ob.h[:, 0:256]), (qk, qk.h[:, 0:256]))
                else:
                    cp(P, "act", (ob, ob.h[:, 0:256]), (pt, pt.h[:, 0:256]))
                dma(P, "sp", (d["qkv"], d["qkv"].h[gt * 128:(gt + 1) * 128, sl * 256:(sl + 1) * 256]), (ob, ob.h[:, 0:256]))

            self.linear_tok(lambda sl: (d["r_w"], d["r_w"].h[ri, sl]), 96, TB // 128, epi,
                            lambda kc, t: self.at_split(kc, t * 128, 128))
        if prev is not None:
            self.flip_x()

        cst, sm = s["cst"], s["small"]
        zt = s["z"]
        zv = [P.view(zt, zt.h[:, 0:4096], f"zA{li}"), P.view(zt, zt.h[:, 4096:8192], f"zB{li}")]
        P.split(zt, zv)
        z16 = [v.h[:, :].bitcast(BF16) for v in zv]
        m16 = s["misc16"]
        MV = [P.view(m16, m16.h[:, 0:768], f"m16A{li}"), P.view(m16, m16.h[:, 768:1536], f"m16B{li}")]
        P.split(m16, MV)
        ogp = s["misc16b"]
        OG = [P.view(ogp, ogp.h[:, 0:512], f"ogA{li}"), P.view(ogp, ogp.h[:, 512:1024], f"ogB{li}")]
        P.split(ogp, OG)
        SMV = [P.view(sm, sm.h[:, 128:144], f"smA{li}"), P.view(sm, sm.h[:, 144:160], f"smB{li}")]
        gngb = P.view(sm, sm.h[:, 0:64], f"gng{li}")
        P.split(sm, SMV + [gngb])
        dma(P, "sp", W(gngb), (d["r_gn"], d["r_gn"].h[ri]))
        dma(P, "sp", (cst, cst.h[:, 1024:1040]), W(d["r_zeta"]))
        dma(P, "sp", (cst, cst.h[:, 1040:1056]), W(d["r_dc"]))
        qk_ = d["qkv"]

        def load_chunk(hg, n):
            zb, zz = zv[n % 2], z16[n % 2]
            rows = slice(n * 128, (n + 1) * 128)
            dma(P, "sp", (zb, zz[:, 0:1024]), (qk_, qk_.h[rows, hg * 1024:(hg + 1) * 1024]))
            dma(P, "sp", (zb, zz[:, 1024:2048]), (qk_, qk_.h[rows, 4096 + hg * 1024:4096 + (hg + 1) * 1024]))
            dma(P, "sp", (zb, zz[:, 2048:4096]), (qk_, qk_.h[rows, 8192 + hg * 2048:8192 + (hg + 1) * 2048]))
            dma(P, "sp", (zb, zz[:, 4096:6144]), (qk_, qk_.h[rows, 16384 + hg * 2048:16384 + (hg + 1) * 2048]))

        for hg in range(4):
            dma(P, "sp", (cst, cst.h[:, 0:512].rearrange("p (h i) -> p h i", h=4)), (d["r_dt"], d["r_dt"].h[:, hg * 4:(hg + 1) * 4, :]))
            dma(P, "sp", (cst, cst.h[:, 512:1024].rearrange("p (h i) -> p h i", h=4)), (d["r_xi"], d["r_xi"].h[:, hg * 4:(hg + 1) * 4, :]))
            for sb_ in (s["stA"], s["stB"]):
                memset(P, "pool", W(sb_), 0.0)
            items = [(n, hl) for n in range(NT) for hl in range(4)]
            NI = len(items)

            def ctx(k):
                n, hl = items[k]
                par = k % 2
                zb, zz = zv[n % 2], z16[n % 2]
                return dict(n=n, hl=hl, h=hg * 4 + hl, par=par, zb=zb,
                            qh=zz[:, hl * 256:(hl + 1) * 256], kh=zz[:, 1024 + hl * 256:1024 + (hl + 1) * 256],
                            vh=(zb, zz[:, 2048 + hl * 512:2048 + (hl + 1) * 512]),
                            gh=(zb, zz[:, 4096 + hl * 512:4096 + (hl + 1) * 512]),
                            stf=s["stA"] if hl < 2 else s["stB"], stb=s["sbA"] if hl < 2 else s["sbB"], so=(hl % 2) * 1024,
                            M=MV[par], AT_=s["sb16"][par], kz=s["sb16"][2 + par], wv=s["sb16"][4 + par],
                            pS=self.ps[par], pO=self.ps[2 + par], on=s["st"][par], sg=s["st"][2 + par],
                            smv=SMV[par], og=OG[par])

            def S1(c):
                zb, M, hl, h = c["zb"], c["M"], c["hl"], c["h"]
                ptt = self.ps[6]
                p16 = ptt.h[:, :].bitcast(BF16)
                for dc in range(2):
                    tr(P, (ptt, p16[:, dc * 128:(dc + 1) * 128]), (zb, c["qh"][:, dc * 128:(dc + 1) * 128]), W(s["identb"]))
                    tr(P, (ptt, p16[:, 256 + dc * 128:256 + (dc + 1) * 128]), (zb, c["kh"][:, dc * 128:(dc + 1) * 128]), W(s["identb"]))
                cp(P, "act", (M, M.h[:, 0:256]), (ptt, p16[:, 0:256]))
                cp(P, "act", (M, M.h[:, 512:768]), (ptt, p16[:, 256:512]))
                tt(P, "dve", (M, M.h[:, 256:512].rearrange("p (c i) -> p c i", c=2)), (M, M.h[:, 0:256].rearrange("p (c i) -> p c i", c=2)),
                   (cst, cst.h[:, 512 + hl * 128:512 + (hl + 1) * 128].unsqueeze(1).to_broadcast([128, 2, 128])), ALU.mult)
                pS = c["pS"]
                for dc in range(2):
                    mm(P, (pS, pS.h[:, 0:128]), (M, M.h[:, 512 + dc * 128:512 + (dc + 1) * 128]), (M, M.h[:, dc * 128:(dc + 1) * 128]), dc == 0, dc == 1)
                AT_ = c["AT_"]
                tt(P, "dve", (AT_, AT_.h[:, 0:128]), (pS, pS.h[:, 0:128]), (cst, cst.h[:, hl * 128:(hl + 1) * 128]), ALU.mult)
                kz = c["kz"]
                ts(P, "pool", (kz, kz.h[:, 0:256]), (zb, c["kh"]), (cst, cst.h[:, 1024 + h:1025 + h]), ALU.mult)

            def S2(c):
                M, n, h, so, stf, stb = c["M"], c["n"], c["h"], c["so"], c["stf"], c["stb"]
                AT_, kz, pO, vh = c["AT_"], c["kz"], c["pO"], c["vh"]
                mm(P, (pO, pO.h[:, :]), (AT_, AT_.h[:, 0:128]), vh, True, n == 0)
                if n > 0:
                    for dc in range(2):
                        mm(P, (pO, pO.h[:, :]), (M, M.h[:, 256 + dc * 128:256 + (dc + 1) * 128]),
                           (stb, stb.h[:, so + dc * 512:so + (dc + 1) * 512]), False, dc == 1)
                for dc in range(2):
                    pD = self.ps[4 + dc]
                    mm(P, (pD, pD.h[:, :]), (kz, kz.h[:, dc * 128:(dc + 1) * 128]), vh, True, True)
                    sf = (stf, stf.h[:, so + dc * 512:so + (dc + 1) * 512])
                    stt(P, sf, sf, (cst, cst.h[:, 1040 + h:1041 + h]), (pD, pD.h[:, :]), ALU.mult, ALU.add)
                    cp(P, "act", (stb, stb.h[:, so + dc * 512:so + (dc + 1) * 512]), sf)
                smv = c["smv"]
                stats = (smv, smv.h[:, 0:6])
                mv = (smv, smv.h[:, 6:8])
                rs = (smv, smv.h[:, 8:9])
                nb = (smv, smv.h[:, 9:10])
                P.op("dve", lambda e, o=stats[1], i=pO.h[:, :]: e.bn_stats(o, i), r=[pO], w=[smv])
                P.op("dve", lambda e, o=mv[1], i=stats[1]: e.bn_aggr(o, i), r=[smv], w=[smv])
                ts(P, "dve", rs, (smv, smv.h[:, 7:8]), GN_EPS, ALU.add)
                act(P, rs, rs, AF.Sqrt)
                P.op("dve", lambda e, o=rs[1], i=rs[1]: e.reciprocal(o, i), r=[smv], w=[smv])
                stt(P, nb, (smv, smv.h[:, 6:7]), -1.0, rs, ALU.mult, ALU.mult)
                on, sg, wv = c["on"], c["sg"], c["wv"]
                act(P, W(on), (pO, pO.h[:, :]), AF.Identity, scale=rs, bias=nb)
                act(P, W(sg), c["gh"], AF.Silu)
                tt(P, "dve", (wv, wv.h[:, 0:512]), W(on), W(sg), ALU.mult)

            def S3(c):
                wv, og, h, n = c["wv"], c["og"], c["h"], c["n"]
                ptw = self.ps[7]
                w16 = ptw.h[:, :].bitcast(BF16)
                for cc in range(4):
                    tr(P, (ptw, w16[:, cc * 128:(cc + 1) * 128]), (wv, wv.h[:, cc * 128:(cc + 1) * 128]), W(s["identb"]))
                for cc in range(4):
                    act(P, (og, og.h[:, cc * 128:(cc + 1) * 128]), (ptw, w16[:, cc * 128:(cc + 1) * 128]), AF.Identity,
                        scale=(gngb, gngb.h[:, h * 4 + cc:h * 4 + cc + 1]))
                dst = d["oT"].h[h * 512:(h + 1) * 512, n * 128:(n + 1) * 128].rearrange("(c p) i -> p c i", p=128)
                dma(P, "sp", (d["oT"], dst), (og, og.h[:, 0:512].rearrange("p (c i) -> p c i", c=4)))

            load_chunk(hg, 0)
            if NT > 1:
                load_chunk(hg, 1)
            for t in range(NI + 2):
                if t < NI:
                    S1(ctx(t))
                if 1 <= t <= NI:
                    S2(ctx(t - 1))
                    n_done, hl_done = items[t - 1]
                    if hl_done == 3 and n_done + 2 < NT:
                        load_chunk(hg, n_done + 2)
                if t >= 2:
                    S3(ctx(t - 2))
        P.join(zt, zv)
        P.join(m16, MV)
        P.join(ogp, OG)
        P.join(sm, SMV + [gngb])

        TBM = self.TBM
        ATa, ATb = s["ATa"], s["ATb"]
        for blk in range(S // TBM):
            tok0 = blk * TBM
            for (ab, r0) in ((ATa, 0), (ATb, 4096)):
                src = d["oT"].h[r0:r0 + 4096, tok0:tok0 + TBM].rearrange("(k p) t -> p k t", p=128)
                dma(P, "sp", (ab, ab.h[:, 0:32 * TBM].rearrange("p (k t) -> p k t", k=32)), (d["oT"], src))

            def epi_o(cc, b, pt, tok0=tok0):
                ob = s["st"][2 + cc % 2]
                cp(P, "act", (ob, ob.h[:, 0:TBM]), (pt, pt.h[:, 0:TBM]))
                dma(P, "sp", (d["yT"], d["yT"].h[cc * 128:(cc + 1) * 128, tok0:tok0 + TBM]), (ob, ob.h[:, 0:TBM]))

            def rhs(kc, b):
                ab = ATa if kc < 32 else ATb
                return (ab, ab.h[:, (kc % 32) * TBM:(kc % 32 + 1) * TBM])

            self.linear_feat(lambda sl: (d["r_wo"], d["r_wo"].h[ri, sl]), 32, 128, 1, TBM, 64, epi_o, rhs)

    def moe(self, li, prev):
        P, s, d = self.P, self.s, self.d
        S = self.S
        TB = min(1024, S)
        BW = min(512, TB)
        NB = TB // BW
        self.setup_vectors(li, li, False, prev)
        cst = s["cst"]
        rw32 = s["stB"]
        dma(P, "sp", (rw32, rw32.h[:, 0:KC * NEXP].rearrange("p (k e) -> p k e", k=KC)), (d["m_rw"], d["m_rw"].h[li]))
        rbb = (cst, cst.h[:, 512:512 + NEXP])
        dma(P, "sp", rbb, (d["m_rb"], d["m_rb"].h[li].partition_broadcast(128)))
        bgu = (cst, cst.h[:, 1024:1024 + NEXP * 4])
        dma(P, "sp", bgu, (d["m_bgu"], d["m_bgu"].h[li]))
        bdl, bdh = s["sbA"], s["sbB"]
        dma(P, "pool", (bdl, bdl.h[0:NEXP, :]), (d["m_bdn"], d["m_bdn"].h[li, :, 0:2048]))
        dma(P, "pool", (bdh, bdh.h[0:NEXP, :]), (d["m_bdn"], d["m_bdn"].h[li, :, 2048:4096]))
        ntile = TB // 128
        zt = s["z"]
        z16 = zt.h[:, :].bitcast(BF16)
        sm = s["small"]
        for blk in range(S // TB):
            tok0 = blk * TB

            def router_done(b0, n):
                for tl in range(n // 128):
                    t = b0 // 128 + tl
                    pr = self.ps[6 + tl % 2]
                    tt(P, "dve", (cst, cst.h[:, t * NEXP:(t + 1) * NEXP]), (pr, pr.h[:, 0:NEXP]), rbb, ALU.add)

            self.prologue(lambda kc, b0, n: self.at_split(kc, b0, n) if TB > 512 else (s["ATa"], s["ATa"].h[:, kc * 512 + b0:kc * 512 + b0 + n]),
                          TB, tok0, TB, True, 0, 1, 2, 3, 4, 5, router=(rw32, [self.ps[6], self.ps[7]], router_done))
            cT = s["st"][5]
            cTb = s["sb16"][5]
            for t in range(ntile):
                lg = (cst, cst.h[:, t * NEXP:(t + 1) * NEXP])
                t8 = (sm, sm.h[:, 32:40])
                ex = (sm, sm.h[:, 40:72])
                mk = (sm, sm.h[:, 72:104])
                nm = (sm, sm.h[:, 104:105])
                sm_ = (sm, sm.h[:, 105:106])
                P.op("dve", lambda e, o=t8[1], i=lg[1]: e.max(o, i), r=[cst], w=[sm])
                ts(P, "dve", mk, lg, (sm, sm.h[:, 35:36]), ALU.is_ge)
                ts(P, "dve", nm, (sm, sm.h[:, 32:33]), -1.0, ALU.mult)
                act(P, ex, lg, AF.Exp, bias=nm)
                tt(P, "dve", ex, ex, mk, ALU.mult)
                P.op("dve", lambda e, o=sm_[1], i=ex[1]: e.reduce_sum(o, i, axis=mybir.AxisListType.X), r=[sm], w=[sm])
                P.op("dve", lambda e, o=sm_[1], i=sm_[1]: e.reciprocal(o, i), r=[sm], w=[sm])
                ts(P, "dve", ex, ex, sm_, ALU.mult)
                pt = self.ps[t % 2]
                tr(P, (pt, pt.h[0:NEXP, 0:128]), ex, W(s["ident"]))
                c4 = (t % 4) * 128
                cp(P, "dve", (cT, cT.h[0:NEXP, c4:c4 + 128]), (pt, pt.h[0:NEXP, 0:128]))
                cp(P, "dve", (cTb, cTb.h[0:NEXP, t * 128:(t + 1) * 128]), (cT, cT.h[0:NEXP, c4:c4 + 128]))
                if t % 4 == 3 or t == ntile - 1:
                    t0_ = (t // 4) * 512
                    n_ = (t % 4 + 1) * 128
                    dma(P, "sp", (d["cmbT"], d["cmbT"].h[:, t0_:t0_ + n_]), (cT, cT.h[0:NEXP, 0:n_]))
            for grp in range(4):
                def wfn(sl, grp=grp):
                    el, j = divmod(sl, 2)
                    return (d["m_wgu"], d["m_wgu"].h[li, grp * 8 + el, j])

                def epi_gu(cc, b, pt, grp=grp):
                    sl, hsel = divmod(cc, 2)
                    el, j = divmod(sl, 2)
                    e_ = grp * 8 + el
                    col = 1024 + e_ * 4 + hsel * 2 + j
                    bias = (cst, cst.h[:, col:col + 1])
                    g_ = (s["st"][b], s["st"][b].h[:, 0:BW])
                    if hsel == 0:
                        sg = (s["st"][2], s["st"][2].h[:, 0:BW])
                        ts(P, "dve", g_, (pt, pt.h[:, 0:BW]), bias, ALU.add, 7.0, ALU.min)
                        act(P, sg, g_, AF.Sigmoid, scale=1.702)
                        tt(P, "pool", g_, g_, sg, ALU.mult)
                        if j == 0:
                            cb = s["st"][4 + b]
                            dma(P, "sp", (cb, cb.h[:, 0:BW]),
                                (d["cmbT"], d["cmbT"].h[e_:e_ + 1, b * BW:(b + 1) * BW].partition_broadcast(128)))
                    else:
                        uu = (s["st"][3], s["st"][3].h[:, 0:BW])
                        cb = s["st"][4 + b]
                        ts(P, "dve", uu, (pt, pt.h[:, 0:BW]), bias, ALU.add, 7.0, ALU.min)
                        ts(P, "dve", uu, uu, -7.0, ALU.max, 1.0, ALU.add)
                        tt(P, "dve", uu, uu, g_, ALU.mult)
                        kcl = el * 2 + j
                        c0 = kcl * TB + b * BW
                        tt(P, "dve", (zt, z16[:, c0:c0 + BW]), uu, (cb, cb.h[:, 0:BW]), ALU.mult)

                self.linear_feat(wfn, 16, 256, NB, BW, KC, epi_gu,
                                 lambda kc, b: self.at_split(kc, b * 512, BW) if TB > 512 else (s["ATa"], s["ATa"].h[:, kc * 512:kc * 512 + BW]))

                def extra_dn(cc, b, pt):
                    bd_ = bdl if cc < 16 else bdh
                    co = (cc % 16) * 128
                    mm(P, (pt, pt.h[:, 0:BW]), (bd_, bd_.h[0:NEXP, co:co + 128]), (cTb, cTb.h[0:NEXP, b * BW:(b + 1) * BW]), False, True)

                def epi_dn(cc, b, pt, tok0=tok0, grp=grp):
                    ob = s["st"][cc % 2]
                    c0 = tok0 + b * BW
                    dst = (d["yT"], d["yT"].h[cc * 128:(cc + 1) * 128, c0:c0 + BW])
                    if grp == 0:
                        cp(P, "act", (ob, ob.h[:, 0:BW]), (pt, pt.h[:, 0:BW]))
                    else:
                        pb = s["st"][2 + cc % 2]
                        dma(P, "sp", (pb, pb.h[:, 0:BW]), dst)
                        tt(P, "dve", (ob, ob.h[:, 0:BW]), (pt, pt.h[:, 0:BW]), (pb, pb.h[:, 0:BW]), ALU.add)
                    dma(P, "sp", dst, (ob, ob.h[:, 0:BW]))

                self.linear_feat(lambda sl, grp=grp: (d["m_wdn"], d["m_wdn"].h[li, grp, sl]), 8, 512, NB, BW, 16, epi_dn,
                                 lambda kc, b: (zt, z16[:, kc * TB + b * BW:kc * TB + (b + 1) * BW]),
                                 extra=(extra_dn if grp == 0 else None))
        self.flip_x()

    def final_out(self, li):
        P, s, d = self.P, self.s, self.d
        S = self.S
        prev = dict(li=li, k=1, gate=5)
        mv, mb, ml = d["modv"], d["mod_b"], d["mod_layer"]
        self.load_vec_T([(0, [self.mv(5), (mb, mb.h[5]), (ml, ml.h[li, 5])], True),
                         (4, [(d["ln_gain"], d["ln_gain"].h[li, 1])], False),
                         (5, [(d["ln_bias"], d["ln_bias"].h[li, 1])], False)])
        memset(P, "dve", self.vec(1), 0.0)
        for tok0 in range(0, S, 256):
            self.prologue(None, 0, tok0, min(256, S), True, 0, 1, 2, 3, 4, 5)
        self.flip_x()
        xin = self.xT(self.xcur)
        ot = s["z"]
        for t in range(self.NT):
            half = (t % 2) * 4096
            for g in range(KC // 4):
                ib = s["st"][g % 4]
                src = xin.h[g * 512:(g + 1) * 512, t * 128:(t + 1) * 128].rearrange("(k p) q -> p k q", p=128)
                dma(P, "sp", (ib, ib.h[:, :].rearrange("p (k q) -> p k q", k=4)), (xin, src))
                pt = self.ps[g % 2]
                for j in range(4):
                    tr(P, (pt, pt.h[:, j * 128:(j + 1) * 128]), (ib, ib.h[:, j * 128:(j + 1) * 128]), W(s["ident"]))
                cp(P, "act" if g % 2 == 0 else "dve", (ot, ot.h[:, half + g * 512:half + (g + 1) * 512]), W(pt))
            tok = dma(P, "sp", (d["out"], d["out"].h[t * 128:(t + 1) * 128, :]), (ot, ot.h[:, half:half + 4096]))
            P.finish_on(tok)

    def build(self):
        self.load_consts()
        if self.do_mod:
            self.phase_mod()
        self.phase_rot()
        self.phase_xT()
        prev = None
        ai = ri = 0
        for n, li_real in enumerate(self.layers):
            li = n
            if li_real % 2 == 0:
                self.attention(li, ai, prev)
                bo = (self.d["a_bo"], self.d["a_bo"].h[ai])
                ai += 1
            else:
                self.retention(li, ri, prev)
                bo = None
                ri += 1
            self.moe(li, dict(li=li, k=0, gate=2, bo=bo))
            prev = dict(li=li, k=1, gate=5)
        self.final_out(len(self.layers) - 1)
        return self.P.build()


def _tile_w(w, ncols):
    K, N = w.shape
    return np.ascontiguousarray(w.reshape(K // 128, 128, N // ncols, ncols).transpose(2, 1, 0, 3))


def _kp(v):
    return np.ascontiguousarray(v.reshape(v.shape[:-1] + (KC, 128)))


def _consts(S):
    c = {}
    inv_a = np.float32(500000.0) ** (-(np.arange(0, 16, 2, dtype=np.float32) / np.float32(16)))
    inv_r = np.float32(10000.0) ** (-np.linspace(0.0, 1.0, 128, dtype=np.float32))
    inv = np.concatenate([inv_a, inv_r]).astype(np.float32)
    c["inv"] = np.ascontiguousarray(np.broadcast_to(inv[None, :], (128, 136)))
    j = np.arange(128)[:, None]
    i = np.arange(128)[None, :]
    mask = np.stack([(j > i), (j <= i)], axis=1).astype(np.float32)
    c["a_mask"] = np.ascontiguousarray(mask)
    c["ident"] = np.eye(128, dtype=np.float32)
    h = np.arange(16, dtype=np.float64)
    logd = np.log1p(-np.exp2(-5.0 - h))
    idx = np.arange(128, dtype=np.float64)
    rel = idx[None, :] - idx[:, None]
    dt = np.where(rel[None] >= 0, np.exp(np.maximum(rel, 0.0)[None] * logd[:, None, None]), 0.0) / 16.0
    c["r_dt"] = np.ascontiguousarray(dt.transpose(1, 0, 2)).astype(np.float32)
    xi = np.exp((idx + 1.0)[None, :] * logd[:, None])
    c["r_xi"] = np.ascontiguousarray(np.broadcast_to(xi[None], (128, 16, 128))).astype(np.float32)
    zeta = np.exp((127.0 - idx)[None, :] * logd[:, None]) / 16.0
    c["r_zeta"] = np.ascontiguousarray(zeta.T).astype(np.float32)
    dc = np.exp(128.0 * logd)
    c["r_dc"] = np.ascontiguousarray(np.broadcast_to(dc[None, :], (128, 16))).astype(np.float32)
    return c


def prep_shared(inp, layers, S):
    f = lambda a: np.asarray(a, dtype=np.float32)
    m = {}
    cst = _consts(S)
    m["inv"], m["ident"] = cst["inv"], cst["ident"]
    m["mod_w"] = _tile_w(f(inp["mod_w"]), 256)
    m["mod_b"] = f(inp["mod_b"]).reshape(6, KC, 128)
    m["mod_layer"] = _kp(f(inp["mod_layer"])[layers])
    m["ln_gain"] = _kp(f(inp["ln_gain"])[layers])
    m["ln_bias"] = _kp(f(inp["ln_bias"])[layers])
    al = [i // 2 for i in layers if i % 2 == 0]
    rl = [i // 2 for i in layers if i % 2 == 1]
    if al:
        m["a_mask"] = cst["a_mask"]
        m["a_wqkv"] = np.stack([_tile_w(f(inp["attn_w_qkv"][a]), 256) for a in al])
        m["a_bqkv"] = f(inp["attn_b_qkv"])[al].reshape(len(al), 1, QKV)
        sk = f(inp["attn_sinks"])[al]
        m["a_sink"] = np.stack([np.repeat(s_.reshape(32, 2).T, 64, axis=0) for s_ in sk])
        m["a_wo"] = np.stack([_tile_w(f(inp["attn_w_o"][a]), 256) for a in al])
        m["a_bo"] = _kp(f(inp["attn_b_o"])[al])
    if rl:
        for k in ("r_dt", "r_xi", "r_zeta", "r_dc"):
            m[k] = cst[k]
        m["r_w"] = np.stack([_tile_w(f(inp["ret_w_qkvg"][r]), 256) for r in rl])
        gn = f(inp["ret_gn_gain"])[rl]
        m["r_gn"] = np.ascontiguousarray(gn.reshape(len(rl), 64, 128).transpose(0, 2, 1))
        m["r_wo"] = np.stack([_tile_w(f(inp["ret_w_o"][r]), 128) for r in rl])
    m["m_rw"] = np.stack([np.ascontiguousarray(f(inp["router_w"][i]).reshape(KC, 128, NEXP).transpose(1, 0, 2)) for i in layers])
    m["m_rb"] = f(inp["router_b"])[layers].reshape(len(layers), 1, NEXP)
    m["m_wgu"] = np.stack([np.stack([
        np.ascontiguousarray(f(inp["expert_w_gu"][i][e]).reshape(KC, 128, 2, 2, 128).transpose(3, 1, 0, 2, 4).reshape(2, 128, KC, 256))
        for e in range(NEXP)]) for i in layers])
    m["m_bgu"] = np.stack([np.ascontiguousarray(f(inp["expert_b_gu"][i]).reshape(NEXP, 4, 128).transpose(2, 0, 1).reshape(128, NEXP * 4)) for i in layers])
    m["m_wdn"] = np.stack([np.stack([_tile_w(f(inp["expert_w_down"][i]).reshape(NEXP * EFF, D)[h * 2048:(h + 1) * 2048], 512)
                                     for h in range(4)]) for i in layers])
    m["m_bdn"] = f(inp["expert_b_down"])[layers]
    return m


def prep_core(inp, b, S):
    m = {}
    m["x"] = np.ascontiguousarray(np.asarray(inp["x"][b][:S], dtype=np.float32))
    m["cT"] = np.ascontiguousarray(np.asarray(inp["c"][b], dtype=np.float32).reshape(KC, 128).T)
    m["pos"] = np.ascontiguousarray(np.asarray(inp["positions"][b][:S], dtype=np.int32).reshape(S // 128, 128).T)
    return m


_CACHE = {}


def run_model(inp, S, layers, batches):
    key = (S, tuple(layers))
    if key not in _CACHE:
        _CACHE[key] = Model(S, list(layers)).build()
    nc = _CACHE[key]
    shared = prep_shared(inp, list(layers), S)
    in_maps = []
    for b in batches:
        mcore = dict(shared)
        mcore.update(prep_core(inp, b, S))
        in_maps.append(mcore)
    res = run_bass_kernel_spmd(nc, in_maps, core_ids=list(range(len(batches))))
    return np.stack([np.asarray(r["out"]) for r in res.results])


def kernel(**inputs):
    out = run_model(inputs, 4096, [0, 1, 2, 3], [0, 1])
    return out.astype(np.float32)
```
